# Optimizing a Trainium2 kernel written in Bass

```python
import math
import jax, jax.numpy as jnp
from jax import lax
import numpy as np

D_MODEL = 1024
BATCH = 4
SEQ = 4096
DEPTH = 1

DN_HEADS = 4
DN_HEAD_K = 128
DN_HEAD_V = 128
DN_CONV = 4
DN_CHUNK = 64
DF_HEADS = 4
DF_HEAD = 64
DF_BLOCK = 128
ROPE_THETA = 500000.0
ROPE_DIM = DF_HEAD // 4
FFN_DIM = 2816
FFN_CONV = 3
EPS = 1e-6
POS_OFFSET_MAX = 1024

DN_QK = DN_HEADS * DN_HEAD_K
DN_V = DN_HEADS * DN_HEAD_V
DF_QK = 2 * DF_HEADS * DF_HEAD
DF_V = DF_HEADS * 2 * DF_HEAD
MIX_WIDTH = DN_V + DF_V
IN_SIZES = (DN_QK, DN_QK, DN_V, DN_V, DN_HEADS, DN_HEADS, DF_QK, DF_QK, DF_V)
IN_WIDTH = sum(IN_SIZES)
IN_SPLITS = tuple(sum(IN_SIZES[:i + 1]) for i in range(len(IN_SIZES) - 1))

kernel_name = 'hybrid_deltanet_diffattn_convglu'


def rms_norm(x, w):
    xf = x.astype(jnp.float32)
    y = xf * lax.rsqrt(jnp.mean(xf * xf, axis=-1, keepdims=True) + EPS)
    return (y * w.astype(jnp.float32)).astype(x.dtype)


def l2_norm(x):
    xf = x.astype(jnp.float32)
    return xf * lax.rsqrt(jnp.sum(xf * xf, axis=-1, keepdims=True) + EPS)


def causal_dwconv(x, w, b=None):
    K, C = w.shape
    y = lax.conv_general_dilated(
        x, w[:, None, :].astype(x.dtype), window_strides=(1,), padding=[(K - 1, 0)],
        dimension_numbers=('NWC', 'WIO', 'NWC'), feature_group_count=C)
    return y if b is None else y + b.astype(x.dtype)


def rope_cos_sin(positions):
    inv_freq = ROPE_THETA ** (-jnp.arange(0, ROPE_DIM, 2, dtype=jnp.float32) / ROPE_DIM)
    ang = positions.astype(jnp.float32)[..., None] * inv_freq
    return jnp.cos(ang), jnp.sin(ang)


def partial_rope(x, cos, sin):
    half = ROPE_DIM // 2
    xr = x[..., :ROPE_DIM].astype(jnp.float32)
    x1, x2 = xr[..., :half], xr[..., half:]
    c, s = cos[:, :, None, :], sin[:, :, None, :]
    rot = jnp.concatenate([x1 * c - x2 * s, x2 * c + x1 * s], axis=-1)
    return jnp.concatenate([rot.astype(x.dtype), x[..., ROPE_DIM:]], axis=-1)


def gated_delta_rule(q, k, v, g, beta):
    f32 = jnp.float32
    B_, S_, H, dk = q.shape
    dv = v.shape[-1]
    C = DN_CHUNK
    N = S_ // C

    def chunks(t):
        t = t.astype(f32).reshape((B_, N, C, H) + t.shape[3:])
        return jnp.moveaxis(t, 3, 1)

    q = chunks(q) * (dk ** -0.5)
    k = chunks(k)
    v = chunks(v)
    beta = chunks(beta)
    g = jnp.cumsum(chunks(g), axis=-1)
    tril = jnp.tril(jnp.ones((C, C), dtype=bool))
    strict = jnp.tril(jnp.ones((C, C), dtype=bool), -1)
    decay = jnp.exp(jnp.where(tril, g[..., :, None] - g[..., None, :], -jnp.inf))
    k_beta = k * beta[..., None]
    lower = jnp.where(strict, jnp.einsum('bhnid,bhnjd->bhnij', k_beta, k) * decay, 0.0)
    eye = jnp.eye(C, dtype=f32)
    rhs = jnp.concatenate([v * beta[..., None], k_beta * jnp.exp(g)[..., None]], axis=-1)
    sol = lax.linalg.triangular_solve(eye + lower, rhs, left_side=True, lower=True,
                                      unit_diagonal=True)
    value, k_cumdecay = sol[..., :dv], sol[..., dv:]
    intra = jnp.where(tril, jnp.einsum('bhnid,bhnjd->bhnij', q, k) * decay, 0.0)
    g_last = g[..., -1]
    q_dec = q * jnp.exp(g)[..., None]
    k_dec = k * jnp.exp(g_last[..., None] - g)[..., None]

    def step(state, inp):
        q_i, k_i, intra_i, value_i, kcd_i, gl_i = inp
        v_new = value_i - jnp.einsum('bhck,bhkv->bhcv', kcd_i, state)
        o = (jnp.einsum('bhck,bhkv->bhcv', q_i, state)
             + jnp.einsum('bhij,bhjv->bhiv', intra_i, v_new))
        state = state * jnp.exp(gl_i)[..., None, None] + jnp.einsum('bhck,bhcv->bhkv', k_i, v_new)
        return state, o

    xs = tuple(jnp.moveaxis(t, 2, 0) for t in (q_dec, k_dec, intra, value, k_cumdecay, g_last))
    _, o = lax.scan(step, jnp.zeros((B_, H, dk, dv), f32), xs)
    return jnp.transpose(o, (1, 0, 3, 2, 4)).reshape(B_, S_, H, dv)


def diff_attention(q, k, v, lam):
    f32 = jnp.float32
    B_, S_, H2, d = q.shape
    H = H2 // 2
    dvh = v.shape[-1]
    nb = S_ // DF_BLOCK
    qb = jnp.moveaxis(jnp.transpose(q, (0, 2, 1, 3)).reshape(B_, H2, nb, DF_BLOCK, d), 2, 0)
    kt = jnp.transpose(k, (0, 2, 1, 3))
    vt = jnp.transpose(v, (0, 2, 1, 3))
    key_pos = jnp.arange(S_)
    scale = d ** -0.5

    def block(args):
        q_blk, start = args
        s = jnp.einsum('bhqd,bhkd->bhqk', q_blk, kt).astype(f32) * scale
        q_pos = start + jnp.arange(DF_BLOCK)
        s = jnp.where(key_pos[None, :] <= q_pos[:, None], s, -jnp.inf)
        p = jax.nn.softmax(s, axis=-1).reshape(B_, H, 2, DF_BLOCK, S_)
        a = p[:, :, 0] - lam * p[:, :, 1]
        return jnp.einsum('bhqk,bhkv->bhqv', a.astype(vt.dtype), vt)

    o = lax.map(block, (qb, jnp.arange(nb) * DF_BLOCK))
    return jnp.transpose(o, (1, 0, 3, 2, 4)).reshape(B_, S_, H, dvh)


def setup_inputs(seed: int = 0) -> dict:
    key = jax.random.key(seed)
    ks = jax.random.split(key, 24)
    f32 = jnp.float32
    L = DEPTH

    def nrm(k, shape, scale):
        return jax.random.normal(k, shape, f32) * scale

    def gain(k, n):
        return 1.0 + 0.02 * jax.random.normal(k, (L, n), f32)

    x = jax.random.normal(ks[0], (BATCH, SEQ, D_MODEL), f32)
    offset = jax.random.randint(ks[1], (BATCH, 1), 0, POS_OFFSET_MAX, dtype=jnp.int32)
    positions = offset + jnp.arange(SEQ, dtype=jnp.int32)[None, :]
    dt = jnp.exp(jax.random.uniform(ks[2], (L, DN_HEADS), f32, math.log(1e-3), math.log(1e-1)))
    dn_dt_bias = dt + jnp.log(-jnp.expm1(-dt))
    dn_a_log = jnp.log(jax.random.uniform(ks[3], (L, DN_HEADS), f32, 1.0, 16.0))
    return {
        'x': x,
        'positions': positions,
        'norm1_w': gain(ks[4], D_MODEL),
        'w_in': nrm(ks[5], (L, D_MODEL, IN_WIDTH), D_MODEL ** -0.5),
        'dn_conv_w': nrm(ks[6], (L, DN_CONV, 2 * DN_QK + DN_V), DN_CONV ** -0.5),
        'dn_a_log': dn_a_log,
        'dn_dt_bias': dn_dt_bias,
        'dn_norm_w': gain(ks[7], DN_HEAD_V),
        'df_q_norm_w': gain(ks[8], DF_HEAD),
        'df_k_norm_w': gain(ks[9], DF_HEAD),
        'df_lambda_q1': nrm(ks[10], (L, DF_HEAD), 0.1),
        'df_lambda_k1': nrm(ks[11], (L, DF_HEAD), 0.1),
        'df_lambda_q2': nrm(ks[12], (L, DF_HEAD), 0.1),
        'df_lambda_k2': nrm(ks[13], (L, DF_HEAD), 0.1),
        'df_subln_w': gain(ks[14], 2 * DF_HEAD),
        'w_out': nrm(ks[15], (L, MIX_WIDTH, D_MODEL), MIX_WIDTH ** -0.5),
        'norm2_w': gain(ks[16], D_MODEL),
        'w_up': nrm(ks[17], (L, D_MODEL, 2 * FFN_DIM), D_MODEL ** -0.5),
        'ffn_conv_w': nrm(ks[18], (L, FFN_CONV, 2 * FFN_DIM), FFN_CONV ** -0.5),
        'ffn_conv_b': nrm(ks[19], (L, 2 * FFN_DIM), 0.02),
        'w_down': nrm(ks[20], (L, FFN_DIM, D_MODEL), FFN_DIM ** -0.5),
    }


def reference(x, positions, norm1_w, w_in, dn_conv_w, dn_a_log, dn_dt_bias, dn_norm_w,
              df_q_norm_w, df_k_norm_w, df_lambda_q1, df_lambda_k1, df_lambda_q2,
              df_lambda_k2, df_subln_w, w_out, norm2_w, w_up, ffn_conv_w, ffn_conv_b,
              w_down):
    f32 = jnp.float32
    B_, S_, _ = x.shape
    cos, sin = rope_cos_sin(positions)
    h = x
    for l in range(DEPTH):
        lam_init = 0.8 - 0.6 * math.exp(-0.3 * l)
        proj = rms_norm(h, norm1_w[l]) @ w_in[l]
        dq, dk, dv, dz, da, db, fq, fk, fv = jnp.split(proj, IN_SPLITS, axis=-1)

        qkv = jax.nn.silu(causal_dwconv(jnp.concatenate([dq, dk, dv], axis=-1), dn_conv_w[l]))
        cq, ck, cv = jnp.split(qkv, (DN_QK, 2 * DN_QK), axis=-1)
        q_a = l2_norm(cq.reshape(B_, S_, DN_HEADS, DN_HEAD_K))
        k_a = l2_norm(ck.reshape(B_, S_, DN_HEADS, DN_HEAD_K))
        v_a = cv.reshape(B_, S_, DN_HEADS, DN_HEAD_V)
        g_a = -jnp.exp(dn_a_log[l].astype(f32)) * jax.nn.softplus(
            da.astype(f32) + dn_dt_bias[l].astype(f32))
        beta_a = jax.nn.sigmoid(db.astype(f32))
        o_a = gated_delta_rule(q_a, k_a, v_a, g_a, beta_a).astype(h.dtype)
        o_a = rms_norm(o_a, dn_norm_w[l]) * jax.nn.silu(dz.reshape(B_, S_, DN_HEADS, DN_HEAD_V))

        q_b = partial_rope(rms_norm(fq.reshape(B_, S_, 2 * DF_HEADS, DF_HEAD), df_q_norm_w[l]), cos, sin)
        k_b = partial_rope(rms_norm(fk.reshape(B_, S_, 2 * DF_HEADS, DF_HEAD), df_k_norm_w[l]), cos, sin)
        v_b = fv.reshape(B_, S_, DF_HEADS, 2 * DF_HEAD)
        lam = (jnp.exp(jnp.sum(df_lambda_q1[l].astype(f32) * df_lambda_k1[l].astype(f32)))
               - jnp.exp(jnp.sum(df_lambda_q2[l].astype(f32) * df_lambda_k2[l].astype(f32)))
               + lam_init)
        o_b = diff_attention(q_b, k_b, v_b, lam)
        o_b = rms_norm(o_b, df_subln_w[l]) * (1.0 - lam_init)

        mix = jnp.concatenate([o_a.reshape(B_, S_, DN_V), o_b.reshape(B_, S_, DF_V)], axis=-1)
        h = h + mix @ w_out[l]

        u = causal_dwconv(rms_norm(h, norm2_w[l]) @ w_up[l], ffn_conv_w[l], ffn_conv_b[l])
        gate, up = jnp.split(u, 2, axis=-1)
        h = h + (jax.nn.silu(gate) * up) @ w_down[l]
    return h
```

```python
import contextlib
import numpy as np
import ml_dtypes
import concourse.bass as bass
import concourse.mybir as mybir
from concourse.bass_utils import run_bass_kernel_spmd

F32 = mybir.dt.float32
BF16 = mybir.dt.bfloat16
I32 = mybir.dt.int32
F32R = mybir.dt.float32r
ALU = mybir.AluOpType
AF = mybir.ActivationFunctionType
AX = mybir.AxisListType

EPS = 1e-6
NB = 32
OB0 = 15
NOB = 17
D = 1024
FFN = 2816
LAM_INIT = 0.2
TWO_PI = 6.283185307179586
SEM_LIMIT = 3000


class Buf:
    __slots__ = ("t", "name", "w", "r", "dsem", "dkey", "dcnt", "groups")

    def __init__(self, t, name, parent=None):
        self.t = t
        self.name = name
        self.w = {} if parent is None else parent.w
        self.r = {} if parent is None else parent.r
        self.dsem = None
        self.dkey = None
        self.dcnt = 0


class Sched:
    def __init__(self, nc, es, same_engine_sync=("act", "dve", "pool")):
        self.nc = nc
        self.es = es
        self.eng = {"pe": nc.tensor, "act": nc.scalar, "dve": nc.vector, "pool": nc.gpsimd, "sp": nc.sync}
        self.cur = {}
        self.cnt = {}
        self.known = {e: {} for e in self.eng}
        self.nsem = 0
        self.sems = {}
        self.same = same_engine_sync
        self.dma_tokens = {}
        self.out_tokens = {}
        self.nins = {e: 0 for e in self.eng}
        for e in self.eng:
            self._newsem(e)

    def _alloc(self, name):
        s = self.es.enter_context(self.nc.semaphore(name))
        self.nsem += 1
        self.sems[self.nsem] = s
        return self.nsem

    def _newsem(self, e):
        k = self._alloc("s_%s_%d" % (e, self.nsem))
        self.cur[e] = k
        self.cnt[e] = 0

    def _wait(self, e, need):
        kn = self.known[e]
        for key, val in need.items():
            if kn.get(key, 0) >= val:
                continue
            if key == self.cur[e] and (e == "pe" or e == "sp" or e not in self.same):
                continue
            self.eng[e].wait_ge(self.sems[key], val)
            self.nins[e] += 1
            kn[key] = val

    @staticmethod
    def _merge(need, d):
        for k, v in d.items():
            if need.get(k, 0) < v:
                need[k] = v

    def op(self, e, fn, reads=(), writes=(), nosame=False):
        need = {}
        for b in writes:
            self._merge(need, b.w)
            self._merge(need, b.r)
        if nosame:
            need.pop(self.cur[e], None)
        for b in reads:
            self._merge(need, b.w)
        self._wait(e, need)
        ins = fn()
        if self.cnt[e] >= SEM_LIMIT:
            self._newsem(e)
        self.cnt[e] += 1
        self.nins[e] += 1
        key = self.cur[e]
        ins.then_inc(self.sems[key], 1)
        c = self.cnt[e]
        for b in reads:
            b.r[key] = c
        for b in writes:
            b.w[key] = c
        return ins

    def dma(self, out, in_, reads=(), writes=(), owner=None, q="sp", is_output=False, **kw):
        need = {}
        for b in reads:
            self._merge(need, b.w)
        for b in writes:
            self._merge(need, b.w)
            self._merge(need, b.r)
        self._wait(q, need)
        if owner is None:
            owner = writes[0] if writes else reads[0]
        if owner.dsem is None or owner.dcnt >= SEM_LIMIT * 16:
            owner.dkey = self._alloc("d_%s_%d" % (owner.name, self.nsem))
            owner.dsem = self.sems[owner.dkey]
            owner.dcnt = 0
        ins = self.eng[q].dma_start(out=out, in_=in_, **kw)
        ins.then_inc(owner.dsem, 16)
        owner.dcnt += 16
        self.nins[q] += 1
        for b in reads:
            b.r[owner.dkey] = owner.dcnt
        for b in writes:
            b.w[owner.dkey] = owner.dcnt
        self.dma_tokens[owner.dkey] = owner.dcnt
        if is_output:
            self.out_tokens[owner.dkey] = owner.dcnt
        return ins

    def barrier(self):
        need = dict(self.dma_tokens)
        for e in self.eng:
            if self.cnt[e] > 0:
                need[self.cur[e]] = self.cnt[e]
        for e in self.eng:
            n2 = {k: v for k, v in need.items() if k != self.cur[e] or e not in ("pe", "sp")}
            self._wait(e, n2)

    def finish(self):
        self._wait("sp", dict(self.out_tokens))
        self.barrier()


def build_program(stop_after=None, dbg=None):
    nc = bass.Bass("TRN2", target_bir_lowering=False)
    dbg_out = {}

    def din(name, shape, dt):
        return nc.dram_tensor(name, shape, dt, kind="ExternalInput").ap()

    xl = din("xl", [4096, D], F32)
    posl = din("posl", [1, 4096], I32)
    pval_d = din("pval", [128, 128], BF16)
    pp_d = din("pp", [128, 512], F32)
    cf_d = din("cf", [128, 1408], F32)
    cb_d = din("cb", [128, 1280], BF16)
    nw_d = din("nw", [2, D], F32)
    w_in = din("w_in", [D, 3592], F32)
    w_out = din("w_out", [D, D], F32)
    w_up = din("w_up", [D, 2 * FFN], F32)
    w_down = din("w_down", [FFN, D], F32)
    out_d = nc.dram_tensor("out", [2048, D], F32, kind="ExternalOutput").ap()

    with contextlib.ExitStack() as es:
        S = Sched(nc, es)
        P, A, V, G = nc.tensor, nc.scalar, nc.vector, nc.gpsimd
        cnt = [0]

        def sb(stack, name, shape, dt):
            cnt[0] += 1
            t = stack.enter_context(nc.sbuf_tensor("%s_%d" % (name, cnt[0]), shape, dt))
            return Buf(t, name)

        def pbank(stack, name, dt=F32):
            cnt[0] += 1
            shape = [128, 512] if dt == F32 else [128, 1024]
            t = stack.enter_context(nc.psum_tensor("%s_%d" % (name, cnt[0]), shape, dt))
            return Buf(t, name)

        pp = sb(es, "pp", [128, 512], F32)
        cf = sb(es, "cf", [128, 1408], F32)
        cb = sb(es, "cb", [128, 1280], BF16)
        pval = sb(es, "pval", [128, 128], BF16)
        mixA = sb(es, "mixA", [128, NOB * 512], BF16)
        misc = sb(es, "misc", [128, 16], F32)
        S.dma(pp.t[:], pp_d[:, :], writes=[pp])
        S.dma(cf.t[:], cf_d[:, :], writes=[cf])
        S.dma(cb.t[:], cb_d[:, :], writes=[cb])
        S.dma(pval.t[:], pval_d[:, :], writes=[pval])
        identf = cf.t[:, 0:128]
        trif = cf.t[:, 128:256]
        onesf = cf.t[:, 256:384]
        maskS4 = cf.t[:, 384:896]
        maskIs4 = cf.t[:, 896:1408]
        identb = cb.t[:, 0:128]
        onesb = cb.t[:, 128:256]
        maskIb = cb.t[:, 256:384]
        rotT = cb.t[:, 384:512]
        blk64 = cb.t[:, 512:640]
        ones128 = cb.t[:, 640:768]
        I4b = cb.t[:, 768:1280]

        psf = []
        psb = []

        def set_psum(stack, nf, nb):
            del psf[:]
            del psb[:]
            psf.extend(pbank(stack, "psf%d" % i) for i in range(nf))
            psb.extend(pbank(stack, "psb%d" % i, BF16) for i in range(nb))
        rr = {}

        def nps(pool=None):
            pool = psf if pool is None else pool
            k = id(pool)
            rr[k] = (rr.get(k, -1) + 1) % len(pool)
            return pool[rr[k]]

        def bfv(bank):
            return Buf(bank.t[:, :].bitcast(BF16), bank.name + "_bf", parent=bank)

        def drive(gens, bg=None):
            gens = [g for g in gens if g is not None]
            while gens:
                for g in list(gens):
                    try:
                        next(g)
                    except StopIteration:
                        gens.remove(g)
                if bg is not None and bg[0] is not None:
                    try:
                        next(bg[0])
                    except StopIteration:
                        bg[0] = None

        def drain(g):
            if g is not None:
                for _ in g:
                    pass

        alt = [0]

        def ve():
            alt[0] ^= 1
            return ("dve", V) if alt[0] else ("pool", G)

        S.op("act", lambda: A.activation(out=misc.t[:, 0:4], in_=pp.t[:, 248:252], func=AF.Exp), reads=[pp], writes=[misc])
        S.op("dve", lambda: V.tensor_scalar(out=misc.t[:, 0:4], in0=misc.t[:, 0:4], scalar1=-1.0, scalar2=None, op0=ALU.mult), reads=[misc], writes=[misc])
        lt = sb(es, "lt", [128, 64], F32)
        S.op("dve", lambda: V.tensor_tensor(out=lt.t[:], in0=pp.t[:, 256:320], in1=pp.t[:, 320:384], op=ALU.mult), reads=[pp], writes=[lt])
        S.op("dve", lambda: V.reduce_sum(out=misc.t[:, 5:6], in_=lt.t[:], axis=AX.X), reads=[lt], writes=[misc])
        S.op("dve", lambda: V.tensor_tensor(out=lt.t[:], in0=pp.t[:, 384:448], in1=pp.t[:, 448:512], op=ALU.mult), reads=[pp, misc], writes=[lt])
        S.op("dve", lambda: V.reduce_sum(out=misc.t[:, 6:7], in_=lt.t[:], axis=AX.X), reads=[lt], writes=[misc])
        S.op("act", lambda: A.activation(out=misc.t[:, 5:7], in_=misc.t[:, 5:7], func=AF.Exp), reads=[misc], writes=[misc])
        S.op("dve", lambda: V.scalar_tensor_tensor(out=misc.t[:, 4:5], in0=misc.t[:, 6:7], scalar=-LAM_INIT, in1=misc.t[:, 5:6], op0=ALU.add, op1=ALU.subtract), reads=[misc], writes=[misc])

        def wload(dst, src, nk, col0, ncols, scale_col, stgs, piece=1024, dcol0=0, dstride=None):
            if dstride is None:
                dstride = ncols
            i = 0
            for k in range(nk):
                for a in range(0, ncols, piece):
                    b = min(ncols, a + piece)
                    stg = stgs[i % len(stgs)]
                    i += 1
                    S.dma(stg.t[:, 0:b - a], src[k * 128:(k + 1) * 128, col0 + a:col0 + b], writes=[stg])
                    en, E = ve()
                    o = dst.t[:, k * dstride + dcol0 + a:k * dstride + dcol0 + b]
                    if scale_col is None:
                        S.op(en, lambda E=E, o=o, stg=stg, n=b - a: E.tensor_copy(out=o, in_=stg.t[:, 0:n]), reads=[stg], writes=[dst])
                    else:
                        S.op(en, lambda E=E, o=o, stg=stg, n=b - a, sc=scale_col + k: E.tensor_scalar(out=o, in0=stg.t[:, 0:n], scalar1=pp.t[:, sc:sc + 1], scalar2=None, op0=ALU.mult), reads=[stg, pp], writes=[dst])

        def wb(W, col):
            for lo, hi, b in getattr(W, "groups", None) or ():
                if lo <= col < hi:
                    return b
            return W

        def wdma_g(W, src, nk, col0, dstride, bounds, order):
            W.groups = [(lo, hi, Buf(W.t, "%s_g%d" % (W.name, i))) for i, (lo, hi) in enumerate(bounds)]
            w3 = W.t[:, 0:nk * dstride].rearrange("p (k c) -> p k c", k=nk)
            for gi in order:
                lo, hi, b = W.groups[gi]
                S.dma(w3[:, :, lo:hi], src[0:nk * 128, col0 + lo:col0 + hi].rearrange("(k p) c -> p k c", p=128), writes=[b], q="pool")

        def wdma(dst, src, nk, col0, ncols, dcol0=0, dstride=None):
            if dstride is None:
                dstride = ncols
            d3 = dst.t[:, 0:nk * dstride].rearrange("p (k c) -> p k c", k=nk)[:, :, dcol0:dcol0 + ncols]
            s3 = src[0:nk * 128, col0:col0 + ncols].rearrange("(k p) c -> p k c", p=128)
            S.dma(d3, s3, writes=[dst], q="pool")

        wrow = [None]

        def norm_block(src_ap, xt, sq, ss, xn, pT, dst, dst_ap3, src_reads=()):
            S.dma(xt.t[:], src_ap, reads=list(src_reads), writes=[xt])
            S.op("act", lambda: A.activation(out=sq.t[:], in_=xt.t[:], func=AF.Square, accum_out=ss.t[:, 0:1]), reads=[xt], writes=[sq, ss])
            S.op("act", lambda: A.activation(out=ss.t[:, 1:2], in_=ss.t[:, 0:1], func=AF.Ln, bias=EPS, scale=1.0 / D), reads=[ss], writes=[ss])
            S.op("act", lambda: A.activation(out=ss.t[:, 2:3], in_=ss.t[:, 1:2], func=AF.Exp, scale=-0.5), reads=[ss], writes=[ss])
            S.op("dve", lambda: V.scalar_tensor_tensor(out=xn.t[:], in0=xt.t[:], scalar=ss.t[:, 2:3], in1=wrow[0].t[:], op0=ALU.mult, op1=ALU.mult), reads=[xt, ss, wrow[0]], writes=[xn])
            for k in range(8):
                S.op("pe", lambda k=k: P.transpose(out=pT.t[:, k * 128:(k + 1) * 128], in_=xn.t[:, k * 128:(k + 1) * 128], identity=identb), reads=[xn, cb], writes=[pT])
            S.op("act", lambda: A.activation(out=dst_ap3, in_=pT.t[:, :].rearrange("p (k t) -> p k t", k=8), func=AF.Copy), reads=[pT], writes=[dst])

        def xtile(tt, xbufs, xnT):
            for j in range(4):
                blk = tt * 4 + j
                xt, sq, ss, xn = xbufs[j % len(xbufs)]
                dst3 = xnT.t[:, :].rearrange("p (k t) -> p k t", k=8)[:, :, j * 128:(j + 1) * 128]
                norm_block(xl[blk * 128:(blk + 1) * 128, :], xt, sq, ss, xn, psb[j % 2], xnT, dst3)

        def proj_fm(ps, W, wstride, wcol, xnT):
            for k in range(8):
                S.op("pe", lambda k=k: P.matmul(ps.t[:, :], lhsT=W.t[:, k * wstride + wcol:k * wstride + wcol + 128], rhs=xnT.t[:, k * 512:(k + 1) * 512], start=(k == 0), stop=(k == 7)), reads=[wb(W, wcol), xnT], writes=[ps])

        def proj_tm(ps, n, W, wstride, wcol, xnT, j):
            for k in range(8):
                S.op("pe", lambda k=k: P.matmul(ps.t[:, 0:n], lhsT=xnT.t[:, k * 512 + j * 128:k * 512 + (j + 1) * 128], rhs=W.t[:, k * wstride + wcol:k * wstride + wcol + n], start=(k == 0), stop=(k == 7)), reads=[wb(W, wcol), xnT], writes=[ps])

        with contextlib.ExitStack() as ph:
            set_psum(ph, 8, 0)
            Wdn = sb(ph, "Wdn", [128, 8 * 2056], BF16)
            kTt = [sb(ph, "kT%d" % i, [128, 2048], BF16) for i in range(2)]
            vtokt = [sb(ph, "vtok%d" % i, [128, 2048], BF16) for i in range(2)]
            qTt = [sb(ph, "qT%d" % i, [128, 2048], BF16) for i in range(2)]
            zsTt = [sb(ph, "zsT%d" % i, [128, 2048], BF16) for i in range(2)]
            gbuf = sb(ph, "g", [128, NB * 4], F32)
            bbuf = sb(ph, "beta", [128, NB * 4], F32)
            xnTs = [sb(ph, "xnT%d" % i, [128, 8 * 512], BF16) for i in range(1)]
            _sq = sb(ph, "sq", [128, D], BF16)
            _xn = sb(ph, "xn", [128, D], BF16)
            xbufs = [(sb(ph, "xt%d" % i, [128, D], F32), _sq, sb(ph, "ss%d" % i, [128, 4], F32), _xn) for i in range(2)]
            wrow[0] = sb(ph, "w1r", [128, D], F32)
            S.dma(wrow[0].t[:], nw_d[0:1, :].partition_broadcast(128), writes=[wrow[0]])
            stg = [sb(ph, "stg%d" % i, [128, 515], F32) for i in range(2)]
            acc = [sb(ph, "acc%d" % i, [128, 512], F32) for i in range(2)]
            sil = [sb(ph, "sil%d" % i, [128, 512], F32) for i in range(2)]
            silb = [sb(ph, "silb%d" % i, [128, 512], BF16) for i in range(2)]
            sqb = [sb(ph, "sqb%d" % i, [128, 512], BF16) for i in range(2)]
            rsb = [sb(ph, "rsb%d" % i, [128, 512], F32) for i in range(2)]
            halo = sb(ph, "halo", [128, 12 * 3], F32)
            small = sb(ph, "small", [128, 4 * 24], F32)

            S.op("pool", lambda: G.memset(halo.t[:], 0.0), writes=[halo])
            wdma_g(Wdn, w_in, 8, 0, 2056, [(0, 512), (512, 1024), (1024, 1536), (1536, 2056)], order=[3, 1, 2, 0])

            PA = [[psf[0]], [psf[1]]]
            PPs = [[psf[2], psf[3]], [psf[4], psf[5]]]
            PSC = [psf[6], psf[7]]

            def projX(tt):
                xnT = xnTs[tt % len(xnTs)]
                for j in range(4):
                    blk = tt * 4 + j
                    xt_, sq_x, ss_x, xn_x = xbufs[j % len(xbufs)]
                    dst3 = xnT.t[:, :].rearrange("p (k t) -> p k t", k=8)[:, :, j * 128:(j + 1) * 128]
                    norm_block(xl[blk * 128:(blk + 1) * 128, :], xt_, sq_x, ss_x, xn_x, bfv(nps(PA[j % 2])), xnT, dst3)
                    yield
                ps = nps(PA[0])
                for j in range(4):
                    for k in range(8):
                        S.op("pe", lambda k=k, j=j: P.matmul(ps.t[:, j * 8:j * 8 + 8], lhsT=xnT.t[:, k * 512 + j * 128:k * 512 + (j + 1) * 128], rhs=Wdn.t[:, k * 2056 + 2048:k * 2056 + 2056], start=(k == 0), stop=(k == 7)), reads=[wb(Wdn, 2048), xnT], writes=[ps])
                ps3 = ps.t[:, 0:32].rearrange("p (j c) -> p j c", c=8)
                sm3 = small.t[:, :].rearrange("p (s j c) -> p s j c", j=4, c=4)
                dtb = pp.t[:, 252:256]
                for j in range(4):
                    S.op("dve", lambda j=j: V.tensor_tensor(out=small.t[:, j * 4:j * 4 + 4], in0=ps.t[:, j * 8:j * 8 + 4], in1=dtb, op=ALU.add), reads=[ps, pp], writes=[small])
                S.op("act", lambda: A.activation(out=small.t[:, 16:32], in_=small.t[:, 0:16], func=AF.Exp), reads=[small], writes=[small])
                S.op("act", lambda: A.activation(out=small.t[:, 32:48], in_=small.t[:, 16:32], func=AF.Ln, bias=1.0), reads=[small], writes=[small])
                S.op("act", lambda: A.activation(out=sm3[:, 3], in_=ps3[:, :, 4:8], func=AF.Exp, scale=-1.0), reads=[ps], writes=[small])
                S.op("act", lambda: A.activation(out=small.t[:, 48:64], in_=small.t[:, 48:64], func=AF.Ln, bias=1.0), reads=[small], writes=[small])
                S.op("act", lambda: A.activation(out=bbuf.t[:, tt * 16:tt * 16 + 16], in_=small.t[:, 48:64], func=AF.Exp, scale=-1.0), reads=[small], writes=[bbuf])
                for j in range(4):
                    S.op("dve", lambda j=j: V.tensor_tensor(out=gbuf.t[:, tt * 16 + j * 4:tt * 16 + j * 4 + 4], in0=small.t[:, 32 + j * 4:36 + j * 4], in1=misc.t[:, 0:4], op=ALU.mult), reads=[small, misc], writes=[gbuf])
                yield

            def projC(tt, ccs, si):
                xnT = xnTs[tt % len(xnTs)]
                own = tt >= 3
                pool = PA[si]
                kT, vtok, qT, zsT = kTt[tt % 2], vtokt[tt % 2], qTt[tt % 2], zsTt[tt % 2]
                sg, ac, sl, sbv, sq_, rs_ = stg[si], acc[si], sil[si], silb[si], sqb[si], rsb[si]
                for cc in ccs:
                    if cc >= 12:
                        if not own:
                            continue
                        h = cc - 12
                        ps = nps(pool)
                        proj_fm(ps, Wdn, 2056, 1536 + h * 128, xnT)
                        S.op("act", lambda: A.activation(out=sl.t[:], in_=ps.t[:, :], func=AF.Silu), reads=[ps], writes=[sl])
                        d3 = zsT.t[:, :].rearrange("p (b h t) -> p b h t", h=4, t=128)[:, :, h, :]
                        s3 = sl.t[:, :].rearrange("p (b t) -> p b t", t=128)
                        S.op("act", lambda: A.activation(out=d3, in_=s3, func=AF.Copy, scale=pp.t[:, 64:65]), reads=[sl, pp], writes=[zsT])
                        yield
                        continue
                    kind, h = cc // 4, cc % 4
                    if kind == 0 and not own:
                        continue
                    ps = nps(pool)
                    proj_fm(ps, Wdn, 2056, cc * 128, xnT)
                    S.op("act", lambda: A.activation(out=sg.t[:, 3:515], in_=ps.t[:, :], func=AF.Copy), reads=[ps], writes=[sg])
                    S.op("act", lambda: A.activation(out=ac.t[:], in_=ps.t[:, :], func=AF.Copy, scale=pp.t[:, 16 + cc * 4 + 3:16 + cc * 4 + 4]), reads=[ps, pp], writes=[ac])
                    S.op("act", lambda: A.activation(out=sg.t[:, 0:3], in_=halo.t[:, cc * 3:cc * 3 + 3], func=AF.Copy), reads=[halo], writes=[sg])
                    S.op("act", lambda: A.activation(out=halo.t[:, cc * 3:cc * 3 + 3], in_=sg.t[:, 512:515], func=AF.Copy), reads=[sg], writes=[halo])
                    yield
                    wc = 16 + cc * 4
                    for j in range(3):
                        S.op("dve", lambda j=j: V.scalar_tensor_tensor(out=ac.t[:], in0=sg.t[:, j:j + 512], scalar=pp.t[:, wc + j:wc + j + 1], in1=ac.t[:], op0=ALU.mult, op1=ALU.add), reads=[sg, pp, ac], writes=[ac])
                        yield
                    if kind == 2:
                        S.op("act", lambda: A.activation(out=sbv.t[:], in_=ac.t[:], func=AF.Silu), reads=[ac], writes=[sbv])
                        pT = bfv(nps(pool))
                        for j in range(4):
                            S.op("pe", lambda j=j: P.transpose(out=pT.t[:, j * 128:(j + 1) * 128], in_=sbv.t[:, j * 128:(j + 1) * 128], identity=identb), reads=[sbv, cb], writes=[pT])
                        yield
                        bsel = bbuf.t[:, tt * 16:tt * 16 + 16].rearrange("p (j h) -> p j h", h=4)[:, :, h:h + 1].to_broadcast([128, 4, 128])
                        S.op("dve", lambda: V.tensor_tensor(out=vtok.t[:, :].rearrange("p (j h t) -> p j h t", h=4, t=128)[:, :, h, :], in0=pT.t[:, 0:512].rearrange("p (j t) -> p j t", t=128), in1=bsel, op=ALU.mult), reads=[pT, bbuf], writes=[vtok])
                    else:
                        S.op("act", lambda: A.activation(out=sl.t[:], in_=ac.t[:], func=AF.Silu), reads=[ac], writes=[sl])
                        yield
                        S.op("act", lambda: A.activation(out=sq_.t[:], in_=sl.t[:], func=AF.Square), reads=[sl], writes=[sq_])
                        pn = nps(pool)
                        S.op("pe", lambda: P.matmul(pn.t[:, :], lhsT=onesb, rhs=sq_.t[:], start=True, stop=True), reads=[sq_, cb], writes=[pn])
                        yield
                        S.op("act", lambda: A.activation(out=rs_.t[:], in_=pn.t[:, :], func=AF.Ln, bias=EPS, scale=1.0), reads=[pn], writes=[rs_])
                        yield
                        S.op("act", lambda: A.activation(out=rs_.t[:], in_=rs_.t[:], func=AF.Exp, scale=-0.5), reads=[rs_], writes=[rs_])
                        dst = kT if kind == 1 else qT
                        d3 = dst.t[:, :].rearrange("p (b h t) -> p b h t", h=4, t=128)[:, :, h, :]
                        S.op("dve", lambda: V.scalar_tensor_tensor(out=d3, in0=sl.t[:, :].rearrange("p (b t) -> p b t", t=128), scalar=(1.0 if kind == 1 else 128.0 ** -0.5), in1=rs_.t[:, :].rearrange("p (b t) -> p b t", t=128), op0=ALU.mult, op1=ALU.mult), reads=[sl, rs_], writes=[dst])
                    yield

            class Staged:
                def __init__(self, stages):
                    self.stages = [list(st) for st in stages]

                def step(self):
                    while self.stages and not self.stages[0]:
                        self.stages.pop(0)
                    if not self.stages:
                        return False
                    for g in list(self.stages[0]):
                        try:
                            next(g)
                        except StopIteration:
                            self.stages[0].remove(g)
                    return True

                def done(self):
                    return not any(self.stages)

                def drain(self):
                    while self.step():
                        pass

            def projA(tt):
                return Staged([[projX(tt)], [projC(tt, [4, 8, 0, 6, 10, 2, 12, 14], 0), projC(tt, [5, 9, 1, 7, 11, 3, 13, 15], 1)]])

            def drive2(gens, bg):
                gens = [g for g in gens if g is not None]
                while gens:
                    for g in list(gens):
                        try:
                            next(g)
                        except StopIteration:
                            gens.remove(g)
                    if bg is not None:
                        bg.step()
                        bg.step()

            with contextlib.ExitStack() as ph2:
                negm = sb(ph2, "negm", [128, 512], F32)
                S.op("dve", lambda: V.tensor_scalar(out=negm.t[:], in0=maskS4, scalar1=30000.0, scalar2=-30000.0, op0=ALU.mult, op1=ALU.add), reads=[cf], writes=[negm])
                S32 = sb(ph2, "S32", [128, 512], F32)
                Sbf = sb(ph2, "Sbf", [128, 512], BF16)
                S.op("pool", lambda: G.memset(S32.t[:], 0.0), writes=[S32])
                S.op("pool", lambda: G.memset(Sbf.t[:], 0.0), writes=[Sbf])

                def mk(name, n, shape, dt):
                    return [sb(ph2, "%s%d" % (name, i), shape, dt) for i in range(n)]

                diff = mk("diff", 2, [128, 512], F32)
                Erow = mk("Erow", 2, [128, 512], F32)
                t1 = mk("t1", 2, [128, 512], F32)
                t2 = mk("t2", 2, [128, 512], F32)
                Ms = [mk("Ms%d" % p, 2, [128, 512], F32R) for p in range(2)]
                Ls = [mk("Ls%d" % p, 2, [128, 512], F32R) for p in range(2)]
                Qs = [mk("Qs%d" % p, 2, [128, 512], F32R) for p in range(2)]
                sc = mk("sc", 3, [128, 32], F32)
                qdT = mk("qdT", 3, [128, 512], BF16)
                inT = mk("inT", 3, [128, 512], BF16)
                Tt = mk("Tt", 3, [128, 512], BF16)
                kdec = mk("kdec", 3, [128, 512], BF16)
                Xb = mk("Xb", 1, [128, 512], BF16)
                vn = mk("vn", 1, [128, 512], BF16)
                onb = mk("onb", 1, [128, 512], BF16)
                osq = mk("osq", 1, [128, 512], F32)
                xtmp = osq[0]
                oss = mk("oss", 1, [128, 12], F32)

                def prep(c):
                    p = c % 2
                    i3 = c % 3
                    own = c >= OB0
                    pool = PPs[p]
                    s_ = sc[i3]
                    b4 = bbuf.t[:, c * 4:c * 4 + 4]
                    kT, qT = kTt[(c // 4) % 2], qTt[(c // 4) % 2]
                    cj = c % 4
                    kblk = kT.t[:, cj * 512:(cj + 1) * 512]
                    p0 = nps(pool)
                    S.op("pe", lambda: P.matmul(p0.t[:, 0:4], lhsT=trif, rhs=gbuf.t[:, c * 4:c * 4 + 4], start=True, stop=True), reads=[cf, gbuf], writes=[p0])
                    S.op("dve", lambda: V.tensor_copy(out=s_.t[:, 0:4], in_=p0.t[:, 0:4]), reads=[p0], writes=[s_])
                    pg = nps(pool)
                    for h in range(4):
                        S.op("pe", lambda h=h: P.matmul(pg.t[:, h * 128:(h + 1) * 128], lhsT=gbuf.t[:, c * 4 + h:c * 4 + h + 1].to_broadcast([128, 128]), rhs=trif, start=True, stop=True), reads=[cf, gbuf], writes=[pg])
                    glast = pg.t[:, :].rearrange("p (h t) -> p h t", t=128)[:, :, 127]
                    yield
                    S.op("act", lambda: A.activation(out=s_.t[:, 4:8], in_=s_.t[:, 0:4], func=AF.Exp), reads=[s_], writes=[s_])
                    S.op("dve", lambda: V.scalar_tensor_tensor(out=s_.t[:, 8:12], in0=s_.t[:, 4:8], scalar=-1.0, in1=b4, op0=ALU.mult, op1=ALU.mult), reads=[s_, bbuf], writes=[s_])
                    S.op("dve", lambda: V.tensor_tensor(out=s_.t[:, 20:24], in0=glast, in1=s_.t[:, 0:4], op=ALU.subtract), reads=[pg, s_], writes=[s_])
                    S.op("act", lambda: A.activation(out=s_.t[:, 12:16], in_=s_.t[:, 20:24], func=AF.Exp), reads=[s_], writes=[s_])
                    S.op("act", lambda: A.activation(out=s_.t[:, 16:20], in_=glast, func=AF.Exp), reads=[pg], writes=[s_])
                    df_ = diff[p]
                    S.op("dve", lambda: V.tensor_tensor(out=df_.t[:, :].rearrange("p (h t) -> p h t", t=128), in0=pg.t[:, :].rearrange("p (h t) -> p h t", t=128), in1=s_.t[:, 0:4].unsqueeze(2).to_broadcast([128, 4, 128]), op=ALU.subtract), reads=[pg, s_], writes=[df_])
                    S.op("dve", lambda: V.scalar_tensor_tensor(out=df_.t[:], in0=df_.t[:], scalar=0.0, in1=negm.t[:], op0=ALU.min, op1=ALU.add), reads=[df_, negm], writes=[df_])
                    S.op("act", lambda: A.activation(out=df_.t[:], in_=df_.t[:], func=AF.Exp), reads=[df_], writes=[df_])
                    if own:
                        er = Erow[p]
                        S.op("act", lambda: A.activation(out=er.t[:], in_=pg.t[:, :], func=AF.Exp), reads=[pg], writes=[er])
                        S.op("dve", lambda: V.scalar_tensor_tensor(out=qdT[i3].t[:], in0=qT.t[:, cj * 512:(cj + 1) * 512], scalar=1.0, in1=er.t[:], op0=ALU.mult, op1=ALU.mult), reads=[qT, er], writes=[qdT[i3]])
                    yield
                    pb_ = nps(pool)
                    for h in range(4):
                        S.op("pe", lambda h=h: P.matmul(pb_.t[:, h * 128:(h + 1) * 128], lhsT=bbuf.t[:, c * 4 + h:c * 4 + h + 1].to_broadcast([128, 128]), rhs=identf, start=True, stop=True), reads=[cf, bbuf], writes=[pb_])
                    t1_ = t1[p]
                    S.op("dve", lambda: V.tensor_tensor(out=t1_.t[:], in0=df_.t[:], in1=pb_.t[:, :], op=ALU.mult), reads=[df_, pb_], writes=[t1_])
                    pA = nps(pool)
                    for h in range(4):
                        S.op("pe", lambda h=h: P.matmul(pA.t[:, h * 128:(h + 1) * 128], lhsT=kblk[:, h * 128:(h + 1) * 128], rhs=kblk[:, h * 128:(h + 1) * 128], start=True, stop=True), reads=[kT], writes=[pA])
                    M0 = Ms[p][0]
                    S.op("dve", lambda: V.tensor_tensor(out=M0.t[:], in0=pA.t[:, :], in1=t1_.t[:], op=ALU.mult), reads=[pA, t1_], writes=[M0])
                    yield
                    if own:
                        pQ = nps(pool)
                        for h in range(4):
                            S.op("pe", lambda h=h: P.matmul(pQ.t[:, h * 128:(h + 1) * 128], lhsT=kblk[:, h * 128:(h + 1) * 128], rhs=qT.t[:, cj * 512 + h * 128:cj * 512 + (h + 1) * 128], start=True, stop=True), reads=[kT, qT], writes=[pQ])
                        t2_ = t2[p]
                        S.op("dve", lambda: V.tensor_tensor(out=t2_.t[:], in0=df_.t[:], in1=I4b, op=ALU.add), reads=[df_, cb], writes=[t2_])
                        S.op("dve", lambda: V.tensor_tensor(out=inT[i3].t[:], in0=pQ.t[:, :], in1=t2_.t[:], op=ALU.mult), reads=[pQ, t2_], writes=[inT[i3]])
                    pT = nps(pool)
                    for h in range(4):
                        S.op("pe", lambda h=h: P.transpose(out=pT.t[:, h * 128:(h + 1) * 128], in_=M0.t[:, h * 128:(h + 1) * 128].bitcast(F32), identity=identf), reads=[M0, cf], writes=[pT])
                    Lk = Ls[p][0]
                    S.op("act", lambda: A.activation(out=Lk.t[:], in_=pT.t[:, :], func=AF.Copy), reads=[pT], writes=[Lk])
                    Rk = Qs[p][0]
                    S.op("dve", lambda: V.tensor_tensor(out=Rk.t[:], in0=I4b, in1=M0.t[:], op=ALU.subtract), reads=[M0, cb], writes=[Rk])
                    yield
                    Mk = M0
                    for k in range(6):
                        Ln = Ls[p][(k + 1) % 2]
                        pL = nps(pool)
                        for h in range(4):
                            S.op("pe", lambda h=h: P.matmul(pL.t[:, h * 128:(h + 1) * 128], lhsT=Mk.t[:, h * 128:(h + 1) * 128], rhs=Lk.t[:, h * 128:(h + 1) * 128], start=True, stop=True), reads=[Lk, Mk], writes=[pL])
                        if k < 5:
                            Mn = Ms[p][(k + 1) % 2]
                            pM = nps(pool)
                            for h in range(4):
                                S.op("pe", lambda h=h: P.matmul(pM.t[:, h * 128:(h + 1) * 128], lhsT=Lk.t[:, h * 128:(h + 1) * 128], rhs=Mk.t[:, h * 128:(h + 1) * 128], start=True, stop=True), reads=[Lk, Mk], writes=[pM])
                        S.op("act", lambda: A.activation(out=Ln.t[:], in_=pL.t[:, :], func=AF.Copy), reads=[pL], writes=[Ln])
                        if k < 5:
                            S.op("dve", lambda: V.tensor_copy(out=Mn.t[:], in_=pM.t[:, :]), reads=[pM], writes=[Mn])
                        yield
                        pR = nps(pool)
                        for h in range(4):
                            S.op("pe", lambda h=h: P.matmul(pR.t[:, h * 128:(h + 1) * 128], lhsT=Ln.t[:, h * 128:(h + 1) * 128], rhs=Rk.t[:, h * 128:(h + 1) * 128], start=True, stop=True), reads=[Ln, Rk], writes=[pR])
                        Rn = Qs[p][(k + 1) % 2] if k < 5 else Tt[i3]
                        S.op("dve", lambda: V.tensor_tensor(out=Rn.t[:], in0=pR.t[:, :], in1=Rk.t[:], op=ALU.add), reads=[pR, Rk], writes=[Rn])
                        Lk, Rk = Ln, Rn
                        if k < 5:
                            Mk = Mn
                        yield
                    pT2 = bfv(nps(pool))
                    for h in range(4):
                        S.op("pe", lambda h=h: P.transpose(out=pT2.t[:, h * 128:(h + 1) * 128], in_=kblk[:, h * 128:(h + 1) * 128], identity=identb), reads=[kT, cb], writes=[pT2])
                    S.op("dve", lambda: V.tensor_tensor(out=kdec[i3].t[:, :].rearrange("p (h t) -> p h t", t=128), in0=pT2.t[:, 0:512].rearrange("p (h t) -> p h t", t=128), in1=s_.t[:, 12:16].unsqueeze(2).to_broadcast([128, 4, 128]), op=ALU.mult), reads=[pT2, s_], writes=[kdec[i3]])
                    yield

                def scan(c):
                    i3 = c % 3
                    own = c >= OB0
                    ob = c - OB0
                    s_ = sc[i3]
                    kT, vtok, zsT = kTt[(c // 4) % 2], vtokt[(c // 4) % 2], zsTt[(c // 4) % 2]
                    cj = c % 4
                    kblk = kT.t[:, cj * 512:(cj + 1) * 512]
                    pW = nps(PSC)
                    for h in range(4):
                        S.op("pe", lambda h=h: P.matmul(pW.t[:, h * 128:(h + 1) * 128], lhsT=kblk[:, h * 128:(h + 1) * 128], rhs=Sbf.t[:, h * 128:(h + 1) * 128], start=True, stop=True), reads=[kT, Sbf], writes=[pW])
                    X = Xb[0]
                    S.op("dve", lambda: V.tensor_tensor(out=xtmp.t[:, :].rearrange("p (h t) -> p h t", t=128), in0=pW.t[:, :].rearrange("p (h t) -> p h t", t=128), in1=s_.t[:, 8:12].unsqueeze(2).to_broadcast([128, 4, 128]), op=ALU.mult), reads=[pW, s_], writes=[xtmp])
                    S.op("dve", lambda: V.tensor_tensor(out=X.t[:], in0=xtmp.t[:], in1=vtok.t[:, cj * 512:(cj + 1) * 512], op=ALU.add), reads=[xtmp, vtok], writes=[X])
                    yield
                    pV = nps(PSC)
                    for h in range(4):
                        S.op("pe", lambda h=h: P.matmul(pV.t[:, h * 128:(h + 1) * 128], lhsT=Tt[i3].t[:, h * 128:(h + 1) * 128], rhs=X.t[:, h * 128:(h + 1) * 128], start=True, stop=True), reads=[Tt[i3], X], writes=[pV])
                    vn_ = vn[0]
                    S.op("act", lambda: A.activation(out=vn_.t[:], in_=pV.t[:, :], func=AF.Copy), reads=[pV], writes=[vn_])
                    yield
                    if own:
                        pO = nps(PSC)
                        for h in range(4):
                            S.op("pe", lambda h=h: P.matmul(pO.t[:, h * 128:(h + 1) * 128], lhsT=qdT[i3].t[:, h * 128:(h + 1) * 128], rhs=Sbf.t[:, h * 128:(h + 1) * 128], start=True, stop=False), reads=[qdT[i3], Sbf], writes=[pO])
                            S.op("pe", lambda h=h: P.matmul(pO.t[:, h * 128:(h + 1) * 128], lhsT=inT[i3].t[:, h * 128:(h + 1) * 128], rhs=vn_.t[:, h * 128:(h + 1) * 128], start=False, stop=True), reads=[inT[i3], vn_], writes=[pO])
                    pS = nps(PSC)
                    for h in range(4):
                        S.op("pe", lambda h=h: P.matmul(pS.t[:, h * 128:(h + 1) * 128], lhsT=kdec[i3].t[:, h * 128:(h + 1) * 128], rhs=vn_.t[:, h * 128:(h + 1) * 128], start=True, stop=True), reads=[kdec[i3], vn_], writes=[pS])
                    S.op("dve", lambda: V.tensor_tensor(out=S32.t[:, :].rearrange("p (h t) -> p h t", t=128), in0=S32.t[:, :].rearrange("p (h t) -> p h t", t=128), in1=s_.t[:, 16:20].unsqueeze(2).to_broadcast([128, 4, 128]), op=ALU.mult), reads=[S32, s_], writes=[S32])
                    S.op("dve", lambda: V.tensor_tensor(out=S32.t[:], in0=S32.t[:], in1=pS.t[:, :], op=ALU.add), reads=[S32, pS], writes=[S32])
                    S.op("act", lambda: A.activation(out=Sbf.t[:], in_=S32.t[:], func=AF.Copy), reads=[S32], writes=[Sbf])
                    yield
                    if own:
                        os_, oq = oss[0], osq[0]
                        S.op("act", lambda: A.activation(out=oq.t[:], in_=pO.t[:, :], func=AF.Square), reads=[pO], writes=[oq])
                        S.op("dve", lambda: V.reduce_sum(out=os_.t[:, 0:4], in_=oq.t[:, :].rearrange("p (h t) -> p h t", t=128), axis=AX.X), reads=[oq], writes=[os_])
                        S.op("act", lambda: A.activation(out=os_.t[:, 4:8], in_=os_.t[:, 0:4], func=AF.Ln, bias=EPS, scale=1.0 / 128), reads=[os_], writes=[os_])
                        S.op("act", lambda: A.activation(out=os_.t[:, 8:12], in_=os_.t[:, 4:8], func=AF.Exp, scale=-0.5), reads=[os_], writes=[os_])
                        yield
                        on_ = onb[0]
                        S.op("dve", lambda: V.tensor_tensor(out=on_.t[:, :].rearrange("p (h t) -> p h t", t=128), in0=pO.t[:, :].rearrange("p (h t) -> p h t", t=128), in1=os_.t[:, 8:12].unsqueeze(2).to_broadcast([128, 4, 128]), op=ALU.mult), reads=[pO, os_], writes=[on_])
                        pT = bfv(nps(PSC))
                        for h in range(4):
                            S.op("pe", lambda h=h: P.transpose(out=pT.t[:, h * 128:(h + 1) * 128], in_=on_.t[:, h * 128:(h + 1) * 128], identity=identb), reads=[on_, cb], writes=[pT])
                        S.op("dve", lambda: V.tensor_tensor(out=mixA.t[:, ob * 512:(ob + 1) * 512], in0=pT.t[:, 0:512], in1=zsT.t[:, cj * 512:(cj + 1) * 512], op=ALU.mult), reads=[pT, zsT], writes=[mixA])
                    yield

                projA(0).drain()
                bg = projA(1)
                preps = {0: prep(0), 1: prep(1)}
                drain(preps.pop(0))
                for c in range(NB):
                    tt, j = c // 4, c % 4
                    if j == 0 and c > 0:
                        bg = projA(tt + 1) if tt + 1 < 8 else None
                    nxt2 = c + 2
                    gate = nxt2 < NB and nxt2 % 4 == 0
                    req = [scan(c)] + ([preps[c + 1]] if c + 1 in preps else [])
                    while req:
                        for g in list(req):
                            try:
                                next(g)
                            except StopIteration:
                                req.remove(g)
                        if nxt2 < NB and nxt2 not in preps and (not gate or bg is None or bg.done()):
                            preps[nxt2] = prep(nxt2)
                        opt = preps.get(nxt2)
                        if opt is not None:
                            try:
                                next(opt)
                            except StopIteration:
                                pass
                        if bg is not None:
                            bg.step()
                            bg.step()
                    if nxt2 < NB and nxt2 not in preps:
                        if bg is not None:
                            bg.drain()
                        preps[nxt2] = prep(nxt2)
                    if gate:
                        bg = None
                    preps.pop(c + 1, None)
                if bg is not None:
                    bg.drain()
            S.barrier()

        mixB = sb(es, "mixB", [128, NOB * 512], BF16)
        if stop_after == "dn":
            dbg_out["mixA"] = (mixA, [128, NOB * 512], BF16)

        def norm_from(xt, sq, ss, xn, pT, dst, dst_ap3, src3=None):
            S.op("act", lambda: A.activation(out=sq.t[:], in_=xt.t[:], func=AF.Square, accum_out=ss.t[:, 0:1]), reads=[xt], writes=[sq, ss])
            S.op("act", lambda: A.activation(out=ss.t[:, 1:2], in_=ss.t[:, 0:1], func=AF.Ln, bias=EPS, scale=1.0 / D), reads=[ss], writes=[ss])
            S.op("act", lambda: A.activation(out=ss.t[:, 2:3], in_=ss.t[:, 1:2], func=AF.Exp, scale=-0.5), reads=[ss], writes=[ss])
            S.op("dve", lambda: V.scalar_tensor_tensor(out=xn.t[:], in0=xt.t[:], scalar=ss.t[:, 2:3], in1=wrow[0].t[:], op0=ALU.mult, op1=ALU.mult), reads=[xt, ss, wrow[0]], writes=[xn])
            for k in range(8):
                S.op("pe", lambda k=k: P.transpose(out=pT.t[:, k * 128:(k + 1) * 128], in_=xn.t[:, k * 128:(k + 1) * 128], identity=identb), reads=[xn, cb], writes=[pT])
            src = pT.t[:, :].rearrange("p (k t) -> p k t", k=8) if src3 is None else src3
            S.op("act", lambda: A.activation(out=dst_ap3, in_=src, func=AF.Copy), reads=[pT], writes=[dst])

        if stop_after in (None, "df"):
            with contextlib.ExitStack() as ph:
                qTd = sb(ph, "qTd", [128, 4 * 2176], BF16)
                kTd = sb(ph, "kTd", [128, 4 * 4096], BF16)
                Vd = sb(ph, "Vd", [128, NB * 512], BF16)
                with contextlib.ExitStack() as p1:
                    set_psum(p1, 6, 2)
                    Wdf = sb(p1, "Wdf", [128, 8 * 1536], BF16)
                    xnT2 = [sb(p1, "xnT%d" % i, [128, 8 * 512], BF16) for i in range(2)]
                    _sq = sb(p1, "sq", [128, D], BF16)
                    _xn = sb(p1, "xn", [128, D], BF16)
                    xbufs = [(sb(p1, "xt%d" % i, [128, D], F32), _sq, sb(p1, "ss%d" % i, [128, 4], F32), _xn) for i in range(1)]
                    wrow[0] = sb(p1, "w1r", [128, D], F32)
                    S.dma(wrow[0].t[:], nw_d[0:1, :].partition_broadcast(128), writes=[wrow[0]])
                    posi = sb(p1, "posi", [128, 512], I32)
                    ni = sb(p1, "ni", [128, 512], I32)
                    Stab = sb(p1, "Stab", [128, 512], F32)
                    Ctab = sb(p1, "Ctab", [128, 512], F32)
                    pr = [sb(p1, "pr%d" % i, [128, 512], F32) for i in range(2)]
                    sqd = [sb(p1, "sqd%d" % i, [128, 512], BF16) for i in range(2)]
                    rsd = [sb(p1, "rsd%d" % i, [128, 512], F32) for i in range(2)]
                    qn = [sb(p1, "qn%d" % i, [128, 512], BF16) for i in range(2)]
                    u1 = [sb(p1, "u1%d" % i, [128, 512], F32) for i in range(2)]
                    u2 = [sb(p1, "u2%d" % i, [128, 512], F32) for i in range(2)]
                    rr_, r2_, nf = u1[0], u1[1], u2[0]

                    def xtile_gen(tt):
                        xnT_ = xnT2[tt % 2]
                        for j in range(4):
                            blk = tt * 4 + j
                            xt, sq, ss, xn = xbufs[j % len(xbufs)]
                            dst3 = xnT_.t[:, :].rearrange("p (k t) -> p k t", k=8)[:, :, j * 128:(j + 1) * 128]
                            norm_block(xl[blk * 128:(blk + 1) * 128, :], xt, sq, ss, xn, psb[j % 2], xnT_, dst3)
                            yield

                    drain(xtile_gen(0))
                    wdma_g(Wdf, w_in, 8, 2056, 1536, [(0, 512), (512, 1024), (1024, 1536)], order=[1, 2, 0])
                    for tt in range(8):
                        xnT = xnT2[tt % 2]
                        S.dma(posi.t[:], posl[0:1, tt * 512:(tt + 1) * 512].partition_broadcast(128), writes=[posi])
                        S.op("dve", lambda: V.tensor_copy(out=rr_.t[:], in_=posi.t[:]), reads=[posi], writes=[rr_])
                        S.op("dve", lambda: V.tensor_scalar(out=rr_.t[:], in0=rr_.t[:], scalar1=pp.t[:, 68:69], scalar2=1.0 / TWO_PI, op0=ALU.mult, op1=ALU.mult), reads=[rr_, pp], writes=[rr_])
                        for (tab, shift) in ((Stab, 0.0), (Ctab, 0.25)):
                            S.op("dve", lambda shift=shift: V.tensor_scalar(out=r2_.t[:], in0=rr_.t[:], scalar1=shift, scalar2=None, op0=ALU.add), reads=[rr_], writes=[r2_])
                            S.op("dve", lambda: V.tensor_copy(out=ni.t[:], in_=r2_.t[:]), reads=[r2_], writes=[ni])
                            S.op("dve", lambda: V.tensor_copy(out=nf.t[:], in_=ni.t[:]), reads=[ni], writes=[nf])
                            S.op("dve", lambda: V.tensor_tensor(out=r2_.t[:], in0=r2_.t[:], in1=nf.t[:], op=ALU.subtract), reads=[r2_, nf], writes=[r2_])
                            S.op("act", lambda tab=tab: A.activation(out=tab.t[:], in_=r2_.t[:], func=AF.Sin, scale=6.2831), reads=[r2_], writes=[tab])
                        def dfchain(ccs, i2, pool):
                            for cc in ccs:
                                isq, hp = cc < 4, cc % 4
                                if isq and tt < 3:
                                    continue
                                ps = nps(pool)
                                proj_fm(ps, Wdf, 1536, cc * 128, xnT)
                                S.op("act", lambda: A.activation(out=sqd[i2].t[:], in_=ps.t[:, :], func=AF.Square), reads=[ps], writes=[sqd[i2]])
                                yield
                                pn = nps(pool)
                                S.op("pe", lambda: P.matmul(pn.t[:, :], lhsT=blk64, rhs=sqd[i2].t[:], start=True, stop=True), reads=[sqd[i2], cb], writes=[pn])
                                S.op("act", lambda: A.activation(out=rsd[i2].t[:], in_=pn.t[:, :], func=AF.Ln, bias=EPS, scale=1.0), reads=[pn], writes=[rsd[i2]])
                                yield
                                S.op("act", lambda: A.activation(out=rsd[i2].t[:], in_=rsd[i2].t[:], func=AF.Exp, scale=-0.5), reads=[rsd[i2]], writes=[rsd[i2]])
                                yield
                                wcol = 65 if isq else 66
                                S.op("dve", lambda: V.scalar_tensor_tensor(out=qn[i2].t[:], in0=ps.t[:, :], scalar=pp.t[:, wcol:wcol + 1], in1=rsd[i2].t[:], op0=ALU.mult, op1=ALU.mult), reads=[ps, pp, rsd[i2]], writes=[qn[i2]])
                                pq = nps(pool)
                                S.op("pe", lambda: P.matmul(pq.t[:, :], lhsT=rotT, rhs=qn[i2].t[:], start=True, stop=True), reads=[qn[i2], cb], writes=[pq])
                                yield
                                S.op("dve", lambda: V.tensor_tensor(out=u1[i2].t[:], in0=pq.t[:, :], in1=Stab.t[:], op=ALU.mult), reads=[pq, Stab], writes=[u1[i2]])
                                S.op("dve", lambda: V.tensor_tensor(out=u2[i2].t[:], in0=qn[i2].t[:], in1=Ctab.t[:], op=ALU.mult), reads=[qn[i2], Ctab], writes=[u2[i2]])
                                yield
                                if isq:
                                    if tt == 3:
                                        o_ap, a0 = qTd.t[:, hp * 2176:hp * 2176 + 128], 384
                                    else:
                                        o_ap, a0 = qTd.t[:, hp * 2176 + 128 + (tt - 4) * 512:hp * 2176 + 128 + (tt - 3) * 512], 0
                                    dstb = qTd
                                else:
                                    o_ap, a0 = kTd.t[:, hp * 4096 + tt * 512:hp * 4096 + (tt + 1) * 512], 0
                                    dstb = kTd
                                S.op("dve", lambda: V.tensor_tensor(out=o_ap, in0=u1[i2].t[:, a0:512], in1=u2[i2].t[:, a0:512], op=ALU.add), reads=[u1[i2], u2[i2]], writes=[dstb])
                                yield

                        def vchain(pool):
                            for j in range(4):
                                blk = tt * 4 + j
                                ps = nps(pool)
                                proj_tm(ps, 512, Wdf, 1536, 1024, xnT, j)
                                S.op("act", lambda: A.activation(out=Vd.t[:, blk * 512:(blk + 1) * 512], in_=ps.t[:, :], func=AF.Copy), reads=[ps], writes=[Vd])
                                yield

                        drive([dfchain([4, 6, 0, 2], 0, [psf[0], psf[1]]), dfchain([5, 7, 1, 3], 1, [psf[2], psf[3]]), vchain([psf[4], psf[5]]), xtile_gen(tt + 1) if tt + 1 < 8 else None])
                S.barrier()
                with contextlib.ExitStack() as p2:
                    set_psum(p2, 8, 0)
                    NPT = 6
                    LA = 3
                    PT = [sb(p2, "PT%d" % i, [128, 512], BF16) for i in range(NPT)]
                    PaccP = [[sb(p2, "PaP%d%d" % (a, m), [128, 512], BF16) for m in range(2)] for a in range(2)]
                    PaccO = [[sb(p2, "PaO%d%d" % (a, m), [128, 512], BF16) for m in range(2)] for a in range(2)]
                    rec = [sb(p2, "rec%d" % i, [128, 512], F32) for i in range(2)]
                    o0 = sb(p2, "o0", [128, 512], F32)
                    o1 = sb(p2, "o1", [128, 512], F32)
                    osq = sb(p2, "osqd", [128, 512], BF16)
                    S.op("dve", lambda: V.memset(mixB.t[:, 0:512], 0.0), writes=[mixB])
                    pvalf = sb(p2, "pvalf", [128, 128], F32)
                    S.op("dve", lambda: V.tensor_copy(out=pvalf.t[:], in_=pval.t[:]), reads=[pval], writes=[pvalf])
                    pOs = [[psf[0], psf[1]], [psf[2], psf[3]]]
                    SPOOL = [psf[4], psf[5], psf[6], psf[7]]
                    groups = []
                    units = []
                    for hp in range(4):
                        for gq in range(5):
                            if gq == 0:
                                q0, N, kd0, nd = 126, 2, 15, 1
                            else:
                                q0, N, kd0, nd = 128 + 512 * (gq - 1), 512, 16 + 4 * (gq - 1), 4
                            gi = len(groups)
                            groups.append((hp, q0, N, kd0, nd))
                            last = kd0 + nd - 1
                            for kb in range(last + 1):
                                off = 0 if kb < kd0 else (kb - kd0) * 128
                                for m in range(2):
                                    units.append((gi, hp, m, kb, off, N, q0, kb == 0, kb == last, kd0))

                    def finalize(gi):
                        hp, q0, N, kd0, nd = groups[gi]
                        par = gi % 2
                        pO = pOs[par]
                        has_own = (kd0 + nd - 1) >= 16
                        for m in range(2):
                            pr_ = nps(SPOOL)
                            S.op("pe", lambda: P.matmul(pr_.t[:, 0:N], lhsT=pval.t[:, :], rhs=PaccP[par][m].t[:, 0:N], start=True, stop=not has_own), reads=[pval, PaccP[par][m]], writes=[pr_])
                            if has_own:
                                S.op("pe", lambda: P.matmul(pr_.t[:, 0:N], lhsT=onesb, rhs=PaccO[par][m].t[:, 0:N], start=False, stop=True), reads=[cb, PaccO[par][m]], writes=[pr_])
                            S.op("act", lambda: A.activation(out=rec[m].t[:, 0:N], in_=pr_.t[:, 0:N], func=AF.Ln, bias=1e-30, scale=1.0), reads=[pr_], writes=[rec[m]])
                            S.op("act", lambda: A.activation(out=rec[m].t[:, 0:N], in_=rec[m].t[:, 0:N], func=AF.Exp, scale=-1.0), reads=[rec[m]], writes=[rec[m]])
                        S.op("dve", lambda: V.tensor_tensor(out=o0.t[:, 0:N], in0=pO[0].t[:, 0:N], in1=rec[0].t[:, 0:N], op=ALU.mult), reads=[pO[0], rec[0]], writes=[o0])
                        S.op("dve", lambda: V.tensor_tensor(out=o1.t[:, 0:N], in0=pO[1].t[:, 0:N], in1=rec[1].t[:, 0:N], op=ALU.mult), reads=[pO[1], rec[1]], writes=[o1])
                        S.op("dve", lambda: V.scalar_tensor_tensor(out=o0.t[:, 0:N], in0=o1.t[:, 0:N], scalar=misc.t[:, 4:5], in1=o0.t[:, 0:N], op0=ALU.mult, op1=ALU.add), reads=[o0, o1, misc], writes=[o0])
                        S.op("act", lambda: A.activation(out=osq.t[:, 0:N], in_=o0.t[:, 0:N], func=AF.Square), reads=[o0], writes=[osq])
                        pn = nps(SPOOL)
                        S.op("pe", lambda: P.matmul(pn.t[:, 0:N], lhsT=ones128, rhs=osq.t[:, 0:N], start=True, stop=True), reads=[osq, cb], writes=[pn])
                        S.op("act", lambda: A.activation(out=rec[0].t[:, 0:N], in_=pn.t[:, 0:N], func=AF.Ln, bias=EPS, scale=1.0), reads=[pn], writes=[rec[0]])
                        S.op("act", lambda: A.activation(out=rec[0].t[:, 0:N], in_=rec[0].t[:, 0:N], func=AF.Exp, scale=-0.5), reads=[rec[0]], writes=[rec[0]])
                        S.op("dve", lambda: V.tensor_scalar(out=o0.t[:, 0:N], in0=o0.t[:, 0:N], scalar1=pp.t[:, 67:68], scalar2=1.0 - LAM_INIT, op0=ALU.mult, op1=ALU.mult), reads=[o0, pp], writes=[o0])
                        ob0 = q0 // 128
                        if N == 2:
                            S.op("dve", lambda: V.tensor_tensor(out=mixB.t[:, hp * 128 + 126:hp * 128 + 128], in0=o0.t[:, 0:2], in1=rec[0].t[:, 0:2], op=ALU.mult), reads=[o0, rec[0]], writes=[mixB])
                        else:
                            d3 = mixB.t[:, :].rearrange("p (b m t) -> p b m t", m=4, t=128)[:, ob0:ob0 + nd, hp, :]
                            S.op("dve", lambda: V.tensor_tensor(out=d3, in0=o0.t[:, 0:N].rearrange("p (b t) -> p b t", t=128), in1=rec[0].t[:, 0:N].rearrange("p (b t) -> p b t", t=128), op=ALU.mult), reads=[o0, rec[0]], writes=[mixB])

                    def stageA(i):
                        gi, hp, m, kb, off, N, q0, first, last, kd0 = units[i]
                        ps = nps(SPOOL)
                        pt = PT[i % NPT]
                        lo, hi = 64 * m, 64 * m + 64
                        S.op("pe", lambda: P.matmul(ps.t[:, off:N], lhsT=kTd.t[lo:hi, hp * 4096 + kb * 128:hp * 4096 + (kb + 1) * 128], rhs=qTd.t[lo:hi, hp * 2176 + q0 + off:hp * 2176 + q0 + N], start=True, stop=True), reads=[kTd, qTd], writes=[ps])
                        S.op("act", lambda: A.activation(out=pt.t[:, off:N], in_=ps.t[:, off:N], func=AF.Exp, scale=0.125), reads=[ps], writes=[pt])
                        en, E = ("dve", V)
                        if kb >= kd0 and N == 2:
                            S.op(en, lambda: E.tensor_tensor(out=pt.t[:, 0:2], in0=pt.t[:, 0:2], in1=maskIb[:, 126:128], op=ALU.mult), reads=[pt, cb], writes=[pt])
                        elif kb >= kd0:
                            S.op(en, lambda: E.tensor_tensor(out=pt.t[:, off:off + 128], in0=pt.t[:, off:off + 128], in1=maskIb, op=ALU.mult), reads=[pt, cb], writes=[pt])
                        par = gi % 2
                        accb = PaccP[par][m] if kb < 16 else PaccO[par][m]
                        if kb == 0 or kb == 16:
                            S.op(en, lambda: E.tensor_copy(out=accb.t[:, 0:N], in_=pt.t[:, 0:N]), reads=[pt], writes=[accb])
                        else:
                            S.op(en, lambda: E.tensor_tensor(out=accb.t[:, off:N], in0=accb.t[:, off:N], in1=pt.t[:, off:N], op=ALU.add), reads=[pt, accb], writes=[accb])

                    def stageC(i):
                        gi, hp, m, kb, off, N, q0, first, last, kd0 = units[i]
                        pt = PT[i % NPT]
                        pO = pOs[gi % 2][m]
                        S.op("pe", lambda: P.matmul(pO.t[:, off:N], lhsT=Vd.t[:, kb * 512 + hp * 128:kb * 512 + (hp + 1) * 128], rhs=pt.t[:, off:N], start=first, stop=last), reads=[Vd, pt], writes=[pO])
                        if last and m == 1:
                            finalize(gi)

                    nu = len(units)
                    LA2 = 4
                    for i in range(0, min(LA2, nu)):
                        stageA(i)
                    for i in range(0, nu, 2):
                        for d in (0, 1):
                            if i + LA2 + d < nu:
                                stageA(i + LA2 + d)
                        stageC(i)
                        stageC(i + 1)
                S.barrier()
            if stop_after == "df":
                dbg_out["mixA"] = (mixA, [128, NOB * 512], BF16)
                dbg_out["mixB"] = (mixB, [128, NOB * 512], BF16)

        if stop_after is None:
            outblk = [Buf(None, "outblk%d" % i) for i in range(16)]
            with contextlib.ExitStack() as phH:
                hnT = sb(phH, "hnT", [128, 8 * 2050], BF16)
                hnT3 = hnT.t[:, :].rearrange("p (k t) -> p k t", k=8)
                Wup = sb(phH, "Wup", [128, 8 * 2816], BF16)
                Wd = sb(phH, "Wd", [128, 11 * 1024], BF16)

                WupG = [Buf(Wup.t, "WupA"), Buf(Wup.t, "WupB")]
                GSPLIT = 6

                def wupb(i):
                    return WupG[0] if i < GSPLIT else WupG[1]

                def ffn_weights(pas):
                    w3 = Wup.t[:, :].rearrange("p (k c) -> p k c", k=8)
                    for gi, (c0, c1) in enumerate(((0, GSPLIT), (GSPLIT, 11))):
                        for base, dcol in ((pas * 1408, 0), (2816 + pas * 1408, 1408)):
                            s3 = w_up[:, base + c0 * 128:base + c1 * 128].rearrange("(k p) c -> p k c", p=128)
                            S.dma(w3[:, :, dcol + c0 * 128:dcol + c1 * 128], s3, writes=[WupG[gi]], q="pool")
                    wdma(Wd, w_down[pas * 1408:(pas + 1) * 1408, :], 11, 0, 1024)

                with contextlib.ExitStack() as ph:
                    set_psum(ph, 6, 2)
                    Wo = sb(ph, "Wo", [128, 8 * 1024], BF16)
                    wrow[0] = sb(ph, "w2r", [128, D], F32)
                    S.dma(wrow[0].t[:], nw_d[1:2, :].partition_broadcast(128), writes=[wrow[0]])
                    xts = [sb(ph, "xt%d" % i, [128, D], F32) for i in range(2)]
                    hts = [sb(ph, "ht%d" % i, [128, D], F32) for i in range(2)]
                    sqs = [sb(ph, "sq%d" % i, [128, D], BF16) for i in range(2)]
                    xns = [sb(ph, "xn%d" % i, [128, D], BF16) for i in range(2)]
                    sss = [sb(ph, "ss%d" % i, [128, 4], F32) for i in range(2)]
                    wdma_g(Wo, w_out, 8, 0, 1024, [(0, 512), (512, 1024)], order=[0, 1])
                    ffn_weights(0)

                    def dblock(ob):
                        par = ob % 2
                        xt, ht, ss, sq, xn, pT = xts[par], hts[par], sss[par], sqs[par], xns[par], psb[par]
                        pool = [psf[3 * par], psf[3 * par + 1], psf[3 * par + 2]]
                        S.dma(xt.t[:], xl[(OB0 + ob) * 128:(OB0 + ob + 1) * 128, :], writes=[xt])
                        for half in range(2):
                            ps = pool[half]
                            for m in range(8):
                                S.op("pe", lambda m=m: P.matmul(ps.t[:, :], lhsT=(mixA if m < 4 else mixB).t[:, (ob * 4 + m % 4) * 128:(ob * 4 + m % 4 + 1) * 128], rhs=Wo.t[:, m * 1024 + half * 512:m * 1024 + (half + 1) * 512], start=(m == 0), stop=(m == 7)), reads=[mixA, mixB, wb(Wo, half * 512)], writes=[ps])
                            S.op("dve", lambda: V.tensor_tensor(out=ht.t[:, half * 512:(half + 1) * 512], in0=ps.t[:, :], in1=xt.t[:, half * 512:(half + 1) * 512], op=ALU.add), reads=[ps, xt], writes=[ht])
                            yield
                        if ob >= 1:
                            S.dma(out_d[(ob - 1) * 128:ob * 128, :], ht.t[:], reads=[ht], writes=[outblk[ob - 1]], owner=ht, is_output=True)
                            dst3, src3 = hnT3[:, :, 2 + (ob - 1) * 128:2 + ob * 128], None
                        else:
                            dst3, src3 = hnT3[:, :, 0:2], pT.t[:, :].rearrange("p (k t) -> p k t", k=8)[:, :, 126:128]
                        S.op("act", lambda: A.activation(out=sq.t[:], in_=ht.t[:], func=AF.Square, accum_out=ss.t[:, 0:1]), reads=[ht], writes=[sq, ss])
                        yield
                        S.op("act", lambda: A.activation(out=ss.t[:, 1:2], in_=ss.t[:, 0:1], func=AF.Ln, bias=EPS, scale=1.0 / D), reads=[ss], writes=[ss])
                        yield
                        S.op("act", lambda: A.activation(out=ss.t[:, 2:3], in_=ss.t[:, 1:2], func=AF.Exp, scale=-0.5), reads=[ss], writes=[ss])
                        S.op("dve", lambda: V.scalar_tensor_tensor(out=xn.t[:], in0=ht.t[:], scalar=ss.t[:, 2:3], in1=wrow[0].t[:], op0=ALU.mult, op1=ALU.mult), reads=[ht, ss, wrow[0]], writes=[xn])
                        yield
                        for k in range(8):
                            S.op("pe", lambda k=k: P.transpose(out=pT.t[:, k * 128:(k + 1) * 128], in_=xn.t[:, k * 128:(k + 1) * 128], identity=identb), reads=[xn, cb], writes=[pT])
                        src = pT.t[:, :].rearrange("p (k t) -> p k t", k=8) if src3 is None else src3
                        S.op("act", lambda: A.activation(out=dst3, in_=src, func=AF.Copy), reads=[pT], writes=[hnT])
                        yield

                    for ob in range(0, NOB, 2):
                        drive([dblock(ob), dblock(ob + 1) if ob + 1 < NOB else None])
                S.barrier()
                with contextlib.ExitStack() as ph:
                    set_psum(ph, 8, 0)
                    ustg = [sb(ph, "ustg%d" % i, [128, 514], F32) for i in range(2)]
                    acc = [sb(ph, "acc%d" % i, [128, 512], F32) for i in range(2)]
                    sgt = sb(ph, "sg", [128, 512], F32)
                    actT = [[sb(ph, "act%d_%d" % (a, i), [128, 512], BF16) for i in range(11)] for a in range(2)]
                    uhalo = sb(ph, "uhalo", [128, 44], F32)
                    hts = [sb(ph, "ht%d" % i, [128, D], F32) for i in range(4)]
                    for pas in range(2):
                        if pas > 0:
                            ffn_weights(pas)
                        def chunk_pair(tt, i):
                            lcs = [i, i + 11]
                            cs = [pas * 11 + i, pas * 11 + i + 22]
                            if tt == 0:
                                for lc in lcs:
                                    ph_ = nps()
                                    for k in range(8):
                                        S.op("pe", lambda k=k: P.matmul(ph_.t[:, 0:2], lhsT=Wup.t[:, k * 2816 + lc * 128:k * 2816 + (lc + 1) * 128], rhs=hnT.t[:, k * 2050:k * 2050 + 2], start=(k == 0), stop=(k == 7)), reads=[wupb(i), hnT], writes=[ph_])
                                    S.op("act", lambda: A.activation(out=uhalo.t[:, lc * 2:lc * 2 + 2], in_=ph_.t[:, 0:2], func=AF.Copy), reads=[ph_], writes=[uhalo])
                            pss = [nps(), nps()]
                            for w in range(2):
                                ps, lc = pss[w], lcs[w]
                                for k in range(8):
                                    S.op("pe", lambda k=k: P.matmul(ps.t[:, :], lhsT=Wup.t[:, k * 2816 + lc * 128:k * 2816 + (lc + 1) * 128], rhs=hnT.t[:, k * 2050 + 2 + tt * 512:k * 2050 + 2 + (tt + 1) * 512], start=(k == 0), stop=(k == 7)), reads=[wupb(i), hnT], writes=[ps])
                            for w in range(2):
                                S.op("act", lambda w=w: A.activation(out=ustg[w].t[:, 2:514], in_=pss[w].t[:, :], func=AF.Copy), reads=[pss[w]], writes=[ustg[w]])
                            for w in range(2):
                                wc, c = 72 + cs[w] * 3, cs[w]
                                S.op("act", lambda w=w, wc=wc, c=c: A.activation(out=acc[w].t[:], in_=pss[w].t[:, :], func=AF.Identity, scale=pp.t[:, wc + 2:wc + 3], bias=pp.t[:, 204 + c:205 + c]), reads=[pss[w], pp], writes=[acc[w]])
                            for w in range(2):
                                lc = lcs[w]
                                S.op("act", lambda w=w, lc=lc: A.activation(out=ustg[w].t[:, 0:2], in_=uhalo.t[:, lc * 2:lc * 2 + 2], func=AF.Copy), reads=[uhalo], writes=[ustg[w]])
                                S.op("act", lambda w=w, lc=lc: A.activation(out=uhalo.t[:, lc * 2:lc * 2 + 2], in_=ustg[w].t[:, 512:514], func=AF.Copy), reads=[ustg[w]], writes=[uhalo])
                            for j in (0, 1):
                                for w in range(2):
                                    wc = 72 + cs[w] * 3
                                    S.op("dve", lambda w=w, wc=wc, j=j: V.scalar_tensor_tensor(out=acc[w].t[:], in0=ustg[w].t[:, j:j + 512], scalar=pp.t[:, wc + j:wc + j + 1], in1=acc[w].t[:], op0=ALU.mult, op1=ALU.add), reads=[ustg[w], pp, acc[w]], writes=[acc[w]])
                            S.op("act", lambda: A.activation(out=sgt.t[:], in_=acc[0].t[:], func=AF.Silu), reads=[acc[0]], writes=[sgt])
                            aT = actT[tt % 2][i]
                            S.op("dve", lambda: V.tensor_tensor(out=aT.t[:], in0=acc[1].t[:], in1=sgt.t[:], op=ALU.mult), reads=[acc[1], sgt], writes=[aT])

                        def wdown_loads(tt):
                            for j in range(4):
                                ob = tt * 4 + j
                                S.dma(hts[j].t[:], out_d[ob * 128:(ob + 1) * 128, :], reads=[outblk[ob]], writes=[hts[j]])

                        def wdown(tt):
                            for j in range(4):
                                ob = tt * 4 + j
                                ht = hts[j]
                                for half in range(2):
                                    ps = nps()
                                    for i in range(11):
                                        aT = actT[tt % 2][i]
                                        S.op("pe", lambda i=i, aT=aT: P.matmul(ps.t[:, :], lhsT=aT.t[:, j * 128:(j + 1) * 128], rhs=Wd.t[:, i * 1024 + half * 512:i * 1024 + (half + 1) * 512], start=(i == 0), stop=(i == 10)), reads=[aT, Wd], writes=[ps])
                                    S.op("dve", lambda: V.tensor_tensor(out=ht.t[:, half * 512:(half + 1) * 512], in0=ps.t[:, :], in1=ht.t[:, half * 512:(half + 1) * 512], op=ALU.add), reads=[ps, ht], writes=[ht])
                                S.dma(out_d[ob * 128:(ob + 1) * 128, :], ht.t[:], reads=[ht], writes=[outblk[ob]], owner=ht, is_output=True)

                        DEFER = 3
                        for tt in range(4):
                            for i in range(11):
                                if tt > 0 and i == 0:
                                    wdown_loads(tt - 1)
                                chunk_pair(tt, i)
                                if tt > 0 and i == DEFER - 1:
                                    wdown(tt - 1)
                                if tt == 3 and i == 7:
                                    wdown_loads(3)
                        wdown(3)

        dbg_d = {}
        for name, (buf, shape, dt) in dbg_out.items():
            d = nc.dram_tensor("dbg_" + name, shape, dt, kind="ExternalOutput").ap()
            S.dma(d[:, :], buf.t[:], reads=[buf], is_output=True)
        if stop_after is not None:
            zt = sb(es, "zt", [128, D], F32)
            S.op("pool", lambda: G.memset(zt.t[:], 0.0), writes=[zt])
            for r in range(16):
                S.dma(out_d[r * 128:(r + 1) * 128, :], zt.t[:], reads=[zt], is_output=True)
        S.finish()
    return nc, S


def build_rest(nc, S, es, L):
    raise NotImplementedError


def _consts():
    p = np.arange(128)
    ident = np.eye(128, dtype=np.float32)
    tri = (p[:, None] <= p[None, :]).astype(np.float32)
    strict = (p[None, :] > p[:, None]).astype(np.float32)
    cf = np.concatenate([ident, tri, np.ones((128, 128), np.float32), np.tile(strict, (1, 4)), np.tile(tri * np.float32(128.0 ** -0.5), (1, 4))], axis=1)
    rotT = np.zeros((128, 128), np.float32)
    for base in (0, 64):
        for i in range(8):
            rotT[base + i + 8, base + i] = -1.0
            rotT[base + i, base + i + 8] = 1.0
    blk64 = np.zeros((128, 128), np.float32)
    blk64[:64, :64] = 1.0 / 64
    blk64[64:, 64:] = 1.0 / 64
    cb = np.concatenate([ident, np.ones((128, 128), np.float32), tri, rotT, blk64, np.full((128, 128), 1.0 / 128, np.float32), np.tile(ident, (1, 4))], axis=1).astype(ml_dtypes.bfloat16)
    return np.ascontiguousarray(cf), np.ascontiguousarray(cb)


def _pack_params(inp):
    pp = np.zeros((128, 512), np.float32)
    pp[:, 0:8] = inp["norm1_w"][0].reshape(8, 128).T
    pp[:, 8:16] = inp["norm2_w"][0].reshape(8, 128).T
    cw = inp["dn_conv_w"][0]
    pp[:, 16:64] = cw.reshape(4, 12, 128).transpose(2, 1, 0).reshape(128, 48)
    pp[:, 64] = inp["dn_norm_w"][0]
    pp[:, 65] = np.tile(inp["df_q_norm_w"][0], 2)
    pp[:, 66] = np.tile(inp["df_k_norm_w"][0], 2)
    pp[:, 67] = inp["df_subln_w"][0]
    invf = np.zeros(128, np.float32)
    base = (500000.0 ** (-np.arange(0, 16, 2, dtype=np.float32) / 16)).astype(np.float32)
    for b0 in (0, 64):
        invf[b0:b0 + 8] = base
        invf[b0 + 8:b0 + 16] = base
    pp[:, 68] = invf
    fw = inp["ffn_conv_w"][0]
    pp[:, 72:204] = fw.reshape(3, 44, 128).transpose(2, 1, 0).reshape(128, 132)
    pp[:, 204:248] = inp["ffn_conv_b"][0].reshape(44, 128).T
    pp[:, 248:252] = inp["dn_a_log"][0][None, :]
    pp[:, 252:256] = inp["dn_dt_bias"][0][None, :]
    pp[:, 256:320] = inp["df_lambda_q1"][0][None, :]
    pp[:, 320:384] = inp["df_lambda_k1"][0][None, :]
    pp[:, 384:448] = inp["df_lambda_q2"][0][None, :]
    pp[:, 448:512] = inp["df_lambda_k2"][0][None, :]
    return pp


def make_in_maps(inp):
    cf, cb = _consts()
    pp = _pack_params(inp)
    x = np.asarray(inp["x"], np.float32)
    pos = np.asarray(inp["positions"], np.int32)
    shared = {
        "pp": pp, "cf": cf, "cb": cb,
        "nw": np.ascontiguousarray(np.stack([inp["norm1_w"][0], inp["norm2_w"][0]]).astype(np.float32)),
        "w_in": np.ascontiguousarray(inp["w_in"][0]), "w_out": np.ascontiguousarray(inp["w_out"][0]),
        "w_up": np.ascontiguousarray(inp["w_up"][0]), "w_down": np.ascontiguousarray(inp["w_down"][0]),
    }
    maps = []
    for c in range(8):
        b, half = c // 2, c % 2
        if half == 1:
            xloc = x[b]
            ploc = pos[b][None, :]
            pv = np.ones((128, 128), ml_dtypes.bfloat16)
        else:
            xloc = np.concatenate([np.zeros((2048, D), np.float32), x[b, :2048]], axis=0)
            ploc = np.concatenate([np.zeros(2048, np.int32), pos[b, :2048]])[None, :]
            pv = np.zeros((128, 128), ml_dtypes.bfloat16)
        m = dict(shared)
        m["xl"] = np.ascontiguousarray(xloc)
        m["posl"] = np.ascontiguousarray(ploc)
        m["pval"] = pv
        maps.append(m)
    return maps


_CACHE = {}


def kernel(**inputs):
    inp = {k: np.asarray(v) for k, v in inputs.items()}
    if "nc" not in _CACHE:
        _CACHE["nc"] = build_program()[0]
    nc = _CACHE["nc"]
    maps = make_in_maps(inp)
    res = run_bass_kernel_spmd(nc, maps, core_ids=list(range(8)))
    out = np.zeros((4, 4096, D), np.float32)
    for c in range(8):
        b, half = c // 2, c % 2
        out[b, half * 2048:(half + 1) * 2048] = res.results[c]["out"]
    return out
```

```python
import contextlib
import numpy as np
import ml_dtypes
import concourse.bass as bass
import concourse.mybir as mybir
from concourse.bass_utils import run_bass_kernel_spmd

F32 = mybir.dt.float32
BF16 = mybir.dt.bfloat16
I32 = mybir.dt.int32
F32R = mybir.dt.float32r
ALU = mybir.AluOpType
AF = mybir.ActivationFunctionType
AX = mybir.AxisListType

EPS = 1e-6
NB = 32
OB0 = 15
NOB = 17
D = 1024
FFN = 2816
LAM_INIT = 0.2
TWO_PI = 6.283185307179586
SEM_LIMIT = 3000


class Buf:
    __slots__ = ("t", "name", "w", "r", "dsem", "dkey", "dcnt", "groups")

    def __init__(self, t, name, parent=None):
        self.t = t
        self.name = name
        self.w = {} if parent is None else parent.w
        self.r = {} if parent is None else parent.r
        self.dsem = None
        self.dkey = None
        self.dcnt = 0


class Sched:
    def __init__(self, nc, es, same_engine_sync=("act", "dve", "pool")):
        self.nc = nc
        self.es = es
        self.eng = {"pe": nc.tensor, "act": nc.scalar, "dve": nc.vector, "pool": nc.gpsimd, "sp": nc.sync}
        self.cur = {}
        self.cnt = {}
        self.known = {e: {} for e in self.eng}
        self.nsem = 0
        self.sems = {}
        self.same = same_engine_sync
        self.dma_tokens = {}
        self.out_tokens = {}
        self.nins = {e: 0 for e in self.eng}
        for e in self.eng:
            self._newsem(e)

    def _alloc(self, name):
        s = self.es.enter_context(self.nc.semaphore(name))
        self.nsem += 1
        self.sems[self.nsem] = s
        return self.nsem

    def _newsem(self, e):
        k = self._alloc("s_%s_%d" % (e, self.nsem))
        self.cur[e] = k
        self.cnt[e] = 0

    def _wait(self, e, need):
        kn = self.known[e]
        for key, val in need.items():
            if kn.get(key, 0) >= val:
                continue
            if key == self.cur[e] and (e == "pe" or e == "sp" or e not in self.same):
                continue
            self.eng[e].wait_ge(self.sems[key], val)
            self.nins[e] += 1
            kn[key] = val

    @staticmethod
    def _merge(need, d):
        for k, v in d.items():
            if need.get(k, 0) < v:
                need[k] = v

    def op(self, e, fn, reads=(), writes=(), nosame=False):
        need = {}
        for b in writes:
            self._merge(need, b.w)
            self._merge(need, b.r)
        if nosame:
            need.pop(self.cur[e], None)
        for b in reads:
            self._merge(need, b.w)
        self._wait(e, need)
        ins = fn()
        if self.cnt[e] >= SEM_LIMIT:
            self._newsem(e)
        self.cnt[e] += 1
        self.nins[e] += 1
        key = self.cur[e]
        ins.then_inc(self.sems[key], 1)
        c = self.cnt[e]
        for b in reads:
            b.r[key] = c
        for b in writes:
            b.w[key] = c
        return ins

    def dma(self, out, in_, reads=(), writes=(), owner=None, q="sp", is_output=False, **kw):
        need = {}
        for b in reads:
            self._merge(need, b.w)
        for b in writes:
            self._merge(need, b.w)
            self._merge(need, b.r)
        self._wait(q, need)
        if owner is None:
            owner = writes[0] if writes else reads[0]
        if owner.dsem is None or owner.dcnt >= SEM_LIMIT * 16:
            owner.dkey = self._alloc("d_%s_%d" % (owner.name, self.nsem))
            owner.dsem = self.sems[owner.dkey]
            owner.dcnt = 0
        ins = self.eng[q].dma_start(out=out, in_=in_, **kw)
        ins.then_inc(owner.dsem, 16)
        owner.dcnt += 16
        self.nins[q] += 1
        for b in reads:
            b.r[owner.dkey] = owner.dcnt
        for b in writes:
            b.w[owner.dkey] = owner.dcnt
        self.dma_tokens[owner.dkey] = owner.dcnt
        if is_output:
            self.out_tokens[owner.dkey] = owner.dcnt
        return ins

    def barrier(self):
        need = dict(self.dma_tokens)
        for e in self.eng:
            if self.cnt[e] > 0:
                need[self.cur[e]] = self.cnt[e]
        for e in self.eng:
            n2 = {k: v for k, v in need.items() if k != self.cur[e] or e not in ("pe", "sp")}
            self._wait(e, n2)

    def finish(self):
        self._wait("sp", dict(self.out_tokens))
        self.barrier()


def build_program(stop_after=None, dbg=None):
    nc = bass.Bass("TRN2", target_bir_lowering=False)
    dbg_out = {}

    def din(name, shape, dt):
        return nc.dram_tensor(name, shape, dt, kind="ExternalInput").ap()

    xl = din("xl", [4096, D], F32)
    posl = din("posl", [1, 4096], I32)
    pval_d = din("pval", [128, 128], BF16)
    pp_d = din("pp", [128, 512], F32)
    cf_d = din("cf", [128, 1408], F32)
    cb_d = din("cb", [128, 1280], BF16)
    nw_d = din("nw", [2, D], F32)
    w_in = din("w_in", [D, 3592], F32)
    w_out = din("w_out", [D, D], F32)
    w_up = din("w_up", [D, 2 * FFN], F32)
    w_down = din("w_down", [FFN, D], F32)
    out_d = nc.dram_tensor("out", [2048, D], F32, kind="ExternalOutput").ap()

    with contextlib.ExitStack() as es:
        S = Sched(nc, es)
        P, A, V, G = nc.tensor, nc.scalar, nc.vector, nc.gpsimd
        cnt = [0]

        def sb(stack, name, shape, dt):
            cnt[0] += 1
            t = stack.enter_context(nc.sbuf_tensor("%s_%d" % (name, cnt[0]), shape, dt))
            return Buf(t, name)

        def pbank(stack, name, dt=F32):
            cnt[0] += 1
            shape = [128, 512] if dt == F32 else [128, 1024]
            t = stack.enter_context(nc.psum_tensor("%s_%d" % (name, cnt[0]), shape, dt))
            return Buf(t, name)

        pp = sb(es, "pp", [128, 512], F32)
        cf = sb(es, "cf", [128, 1408], F32)
        cb = sb(es, "cb", [128, 1280], BF16)
        pval = sb(es, "pval", [128, 128], BF16)
        mixA = sb(es, "mixA", [128, NOB * 512], BF16)
        misc = sb(es, "misc", [128, 16], F32)
        S.dma(pp.t[:], pp_d[:, :], writes=[pp])
        S.dma(cf.t[:], cf_d[:, :], writes=[cf])
        S.dma(cb.t[:], cb_d[:, :], writes=[cb])
        S.dma(pval.t[:], pval_d[:, :], writes=[pval])
        identf = cf.t[:, 0:128]
        trif = cf.t[:, 128:256]
        onesf = cf.t[:, 256:384]
        maskS4 = cf.t[:, 384:896]
        maskIs4 = cf.t[:, 896:1408]
        identb = cb.t[:, 0:128]
        onesb = cb.t[:, 128:256]
        maskIb = cb.t[:, 256:384]
        rotT = cb.t[:, 384:512]
        blk64 = cb.t[:, 512:640]
        ones128 = cb.t[:, 640:768]
        I4b = cb.t[:, 768:1280]

        psf = []
        psb = []

        def set_psum(stack, nf, nb):
            del psf[:]
            del psb[:]
            psf.extend(pbank(stack, "psf%d" % i) for i in range(nf))
            psb.extend(pbank(stack, "psb%d" % i, BF16) for i in range(nb))
        rr = {}

        def nps(pool=None):
            pool = psf if pool is None else pool
            k = id(pool)
            rr[k] = (rr.get(k, -1) + 1) % len(pool)
            return pool[rr[k]]

        def bfv(bank):
            return Buf(bank.t[:, :].bitcast(BF16), bank.name + "_bf", parent=bank)

        def drive(gens, bg=None):
            gens = [g for g in gens if g is not None]
            while gens:
                for g in list(gens):
                    try:
                        next(g)
                    except StopIteration:
                        gens.remove(g)
                if bg is not None and bg[0] is not None:
                    try:
                        next(bg[0])
                    except StopIteration:
                        bg[0] = None

        def drain(g):
            if g is not None:
                for _ in g:
                    pass

        alt = [0]

        def ve():
            alt[0] ^= 1
            return ("dve", V) if alt[0] else ("pool", G)

        S.op("act", lambda: A.activation(out=misc.t[:, 0:4], in_=pp.t[:, 248:252], func=AF.Exp), reads=[pp], writes=[misc])
        S.op("dve", lambda: V.tensor_scalar(out=misc.t[:, 0:4], in0=misc.t[:, 0:4], scalar1=-1.0, scalar2=None, op0=ALU.mult), reads=[misc], writes=[misc])
        lt = sb(es, "lt", [128, 64], F32)
        S.op("dve", lambda: V.tensor_tensor(out=lt.t[:], in0=pp.t[:, 256:320], in1=pp.t[:, 320:384], op=ALU.mult), reads=[pp], writes=[lt])
        S.op("dve", lambda: V.reduce_sum(out=misc.t[:, 5:6], in_=lt.t[:], axis=AX.X), reads=[lt], writes=[misc])
        S.op("dve", lambda: V.tensor_tensor(out=lt.t[:], in0=pp.t[:, 384:448], in1=pp.t[:, 448:512], op=ALU.mult), reads=[pp, misc], writes=[lt])
        S.op("dve", lambda: V.reduce_sum(out=misc.t[:, 6:7], in_=lt.t[:], axis=AX.X), reads=[lt], writes=[misc])
        S.op("act", lambda: A.activation(out=misc.t[:, 5:7], in_=misc.t[:, 5:7], func=AF.Exp), reads=[misc], writes=[misc])
        S.op("dve", lambda: V.scalar_tensor_tensor(out=misc.t[:, 4:5], in0=misc.t[:, 6:7], scalar=-LAM_INIT, in1=misc.t[:, 5:6], op0=ALU.add, op1=ALU.subtract), reads=[misc], writes=[misc])

        def wload(dst, src, nk, col0, ncols, scale_col, stgs, piece=1024, dcol0=0, dstride=None):
            if dstride is None:
                dstride = ncols
            i = 0
            for k in range(nk):
                for a in range(0, ncols, piece):
                    b = min(ncols, a + piece)
                    stg = stgs[i % len(stgs)]
                    i += 1
                    S.dma(stg.t[:, 0:b - a], src[k * 128:(k + 1) * 128, col0 + a:col0 + b], writes=[stg])
                    en, E = ve()
                    o = dst.t[:, k * dstride + dcol0 + a:k * dstride + dcol0 + b]
                    if scale_col is None:
                        S.op(en, lambda E=E, o=o, stg=stg, n=b - a: E.tensor_copy(out=o, in_=stg.t[:, 0:n]), reads=[stg], writes=[dst])
                    else:
                        S.op(en, lambda E=E, o=o, stg=stg, n=b - a, sc=scale_col + k: E.tensor_scalar(out=o, in0=stg.t[:, 0:n], scalar1=pp.t[:, sc:sc + 1], scalar2=None, op0=ALU.mult), reads=[stg, pp], writes=[dst])

        def wb(W, col):
            for lo, hi, b in getattr(W, "groups", None) or ():
                if lo <= col < hi:
                    return b
            return W

        def wdma_g(W, src, nk, col0, dstride, bounds, order):
            W.groups = [(lo, hi, Buf(W.t, "%s_g%d" % (W.name, i))) for i, (lo, hi) in enumerate(bounds)]
            w3 = W.t[:, 0:nk * dstride].rearrange("p (k c) -> p k c", k=nk)
            for gi in order:
                lo, hi, b = W.groups[gi]
                S.dma(w3[:, :, lo:hi], src[0:nk * 128, col0 + lo:col0 + hi].rearrange("(k p) c -> p k c", p=128), writes=[b], q="pool")

        def wdma(dst, src, nk, col0, ncols, dcol0=0, dstride=None):
            if dstride is None:
                dstride = ncols
            d3 = dst.t[:, 0:nk * dstride].rearrange("p (k c) -> p k c", k=nk)[:, :, dcol0:dcol0 + ncols]
            s3 = src[0:nk * 128, col0:col0 + ncols].rearrange("(k p) c -> p k c", p=128)
            S.dma(d3, s3, writes=[dst], q="pool")

        wrow = [None]

        def norm_block(src_ap, xt, sq, ss, xn, pT, dst, dst_ap3, src_reads=()):
            S.dma(xt.t[:], src_ap, reads=list(src_reads), writes=[xt])
            S.op("act", lambda: A.activation(out=sq.t[:], in_=xt.t[:], func=AF.Square, accum_out=ss.t[:, 0:1]), reads=[xt], writes=[sq, ss])
            S.op("act", lambda: A.activation(out=ss.t[:, 1:2], in_=ss.t[:, 0:1], func=AF.Ln, bias=EPS, scale=1.0 / D), reads=[ss], writes=[ss])
            S.op("act", lambda: A.activation(out=ss.t[:, 2:3], in_=ss.t[:, 1:2], func=AF.Exp, scale=-0.5), reads=[ss], writes=[ss])
            S.op("dve", lambda: V.scalar_tensor_tensor(out=xn.t[:], in0=xt.t[:], scalar=ss.t[:, 2:3], in1=wrow[0].t[:], op0=ALU.mult, op1=ALU.mult), reads=[xt, ss, wrow[0]], writes=[xn])
            for k in range(8):
                S.op("pe", lambda k=k: P.transpose(out=pT.t[:, k * 128:(k + 1) * 128], in_=xn.t[:, k * 128:(k + 1) * 128], identity=identb), reads=[xn, cb], writes=[pT])
            S.op("act", lambda: A.activation(out=dst_ap3, in_=pT.t[:, :].rearrange("p (k t) -> p k t", k=8), func=AF.Copy), reads=[pT], writes=[dst])

        def xtile(tt, xbufs, xnT):
            for j in range(4):
                blk = tt * 4 + j
                xt, sq, ss, xn = xbufs[j % len(xbufs)]
                dst3 = xnT.t[:, :].rearrange("p (k t) -> p k t", k=8)[:, :, j * 128:(j + 1) * 128]
                norm_block(xl[blk * 128:(blk + 1) * 128, :], xt, sq, ss, xn, psb[j % 2], xnT, dst3)

        def proj_fm(ps, W, wstride, wcol, xnT):
            for k in range(8):
                S.op("pe", lambda k=k: P.matmul(ps.t[:, :], lhsT=W.t[:, k * wstride + wcol:k * wstride + wcol + 128], rhs=xnT.t[:, k * 512:(k + 1) * 512], start=(k == 0), stop=(k == 7)), reads=[wb(W, wcol), xnT], writes=[ps])

        def proj_tm(ps, n, W, wstride, wcol, xnT, j):
            for k in range(8):
                S.op("pe", lambda k=k: P.matmul(ps.t[:, 0:n], lhsT=xnT.t[:, k * 512 + j * 128:k * 512 + (j + 1) * 128], rhs=W.t[:, k * wstride + wcol:k * wstride + wcol + n], start=(k == 0), stop=(k == 7)), reads=[wb(W, wcol), xnT], writes=[ps])

        with contextlib.ExitStack() as ph:
            set_psum(ph, 8, 0)
            Wdn = sb(ph, "Wdn", [128, 8 * 2056], BF16)
            kTt = [sb(ph, "kT%d" % i, [128, 2048], BF16) for i in range(2)]
            vtokt = [sb(ph, "vtok%d" % i, [128, 2048], BF16) for i in range(2)]
            qTt = [sb(ph, "qT%d" % i, [128, 2048], BF16) for i in range(2)]
            zsTt = [sb(ph, "zsT%d" % i, [128, 2048], BF16) for i in range(2)]
            gbuf = sb(ph, "g", [128, NB * 4], F32)
            bbuf = sb(ph, "beta", [128, NB * 4], F32)
            xnTs = [sb(ph, "xnT%d" % i, [128, 8 * 512], BF16) for i in range(1)]
            _sq = sb(ph, "sq", [128, D], BF16)
            _xn = sb(ph, "xn", [128, D], BF16)
            xbufs = [(sb(ph, "xt%d" % i, [128, D], F32), _sq, sb(ph, "ss%d" % i, [128, 4], F32), _xn) for i in range(2)]
            wrow[0] = sb(ph, "w1r", [128, D], F32)
            S.dma(wrow[0].t[:], nw_d[0:1, :].partition_broadcast(128), writes=[wrow[0]])
            stg = [sb(ph, "stg%d" % i, [128, 515], F32) for i in range(2)]
            acc = [sb(ph, "acc%d" % i, [128, 512], F32) for i in range(2)]
            sil = [sb(ph, "sil%d" % i, [128, 512], F32) for i in range(2)]
            silb = [sb(ph, "silb%d" % i, [128, 512], BF16) for i in range(2)]
            sqb = [sb(ph, "sqb%d" % i, [128, 512], BF16) for i in range(2)]
            rsb = [sb(ph, "rsb%d" % i, [128, 512], F32) for i in range(2)]
            halo = sb(ph, "halo", [128, 12 * 3], F32)
            small = sb(ph, "small", [128, 4 * 24], F32)

            S.op("pool", lambda: G.memset(halo.t[:], 0.0), writes=[halo])
            wdma_g(Wdn, w_in, 8, 0, 2056, [(0, 512), (512, 1024), (1024, 1536), (1536, 2056)], order=[3, 1, 2, 0])

            PA = [[psf[0]], [psf[1]]]
            PPs = [[psf[2], psf[3]], [psf[4], psf[5]]]
            PSC = [psf[6], psf[7]]

            def projX(tt):
                xnT = xnTs[tt % len(xnTs)]
                for j in range(4):
                    blk = tt * 4 + j
                    xt_, sq_x, ss_x, xn_x = xbufs[j % len(xbufs)]
                    dst3 = xnT.t[:, :].rearrange("p (k t) -> p k t", k=8)[:, :, j * 128:(j + 1) * 128]
                    norm_block(xl[blk * 128:(blk + 1) * 128, :], xt_, sq_x, ss_x, xn_x, bfv(nps(PA[j % 2])), xnT, dst3)
                    yield
                ps = nps(PA[0])
                for j in range(4):
                    for k in range(8):
                        S.op("pe", lambda k=k, j=j: P.matmul(ps.t[:, j * 8:j * 8 + 8], lhsT=xnT.t[:, k * 512 + j * 128:k * 512 + (j + 1) * 128], rhs=Wdn.t[:, k * 2056 + 2048:k * 2056 + 2056], start=(k == 0), stop=(k == 7)), reads=[wb(Wdn, 2048), xnT], writes=[ps])
                ps3 = ps.t[:, 0:32].rearrange("p (j c) -> p j c", c=8)
                sm3 = small.t[:, :].rearrange("p (s j c) -> p s j c", j=4, c=4)
                dtb = pp.t[:, 252:256]
                for j in range(4):
                    S.op("dve", lambda j=j: V.tensor_tensor(out=small.t[:, j * 4:j * 4 + 4], in0=ps.t[:, j * 8:j * 8 + 4], in1=dtb, op=ALU.add), reads=[ps, pp], writes=[small])
                S.op("act", lambda: A.activation(out=small.t[:, 16:32], in_=small.t[:, 0:16], func=AF.Exp), reads=[small], writes=[small])
                S.op("act", lambda: A.activation(out=small.t[:, 32:48], in_=small.t[:, 16:32], func=AF.Ln, bias=1.0), reads=[small], writes=[small])
                S.op("act", lambda: A.activation(out=sm3[:, 3], in_=ps3[:, :, 4:8], func=AF.Exp, scale=-1.0), reads=[ps], writes=[small])
                S.op("act", lambda: A.activation(out=small.t[:, 48:64], in_=small.t[:, 48:64], func=AF.Ln, bias=1.0), reads=[small], writes=[small])
                S.op("act", lambda: A.activation(out=bbuf.t[:, tt * 16:tt * 16 + 16], in_=small.t[:, 48:64], func=AF.Exp, scale=-1.0), reads=[small], writes=[bbuf])
                for j in range(4):
                    S.op("dve", lambda j=j: V.tensor_tensor(out=gbuf.t[:, tt * 16 + j * 4:tt * 16 + j * 4 + 4], in0=small.t[:, 32 + j * 4:36 + j * 4], in1=misc.t[:, 0:4], op=ALU.mult), reads=[small, misc], writes=[gbuf])
                yield

            def projC(tt, ccs, si):
                xnT = xnTs[tt % len(xnTs)]
                own = tt >= 3
                pool = PA[si]
                kT, vtok, qT, zsT = kTt[tt % 2], vtokt[tt % 2], qTt[tt % 2], zsTt[tt % 2]
                sg, ac, sl, sbv, sq_, rs_ = stg[si], acc[si], sil[si], silb[si], sqb[si], rsb[si]
                for cc in ccs:
                    if cc >= 12:
                        if not own:
                            continue
                        h = cc - 12
                        ps = nps(pool)
                        proj_fm(ps, Wdn, 2056, 1536 + h * 128, xnT)
                        S.op("act", lambda: A.activation(out=sl.t[:], in_=ps.t[:, :], func=AF.Silu), reads=[ps], writes=[sl])
                        d3 = zsT.t[:, :].rearrange("p (b h t) -> p b h t", h=4, t=128)[:, :, h, :]
                        s3 = sl.t[:, :].rearrange("p (b t) -> p b t", t=128)
                        S.op("act", lambda: A.activation(out=d3, in_=s3, func=AF.Copy, scale=pp.t[:, 64:65]), reads=[sl, pp], writes=[zsT])
                        yield
                        continue
                    kind, h = cc // 4, cc % 4
                    if kind == 0 and not own:
                        continue
                    ps = nps(pool)
                    proj_fm(ps, Wdn, 2056, cc * 128, xnT)
                    S.op("act", lambda: A.activation(out=sg.t[:, 3:515], in_=ps.t[:, :], func=AF.Copy), reads=[ps], writes=[sg])
                    S.op("act", lambda: A.activation(out=ac.t[:], in_=ps.t[:, :], func=AF.Copy, scale=pp.t[:, 16 + cc * 4 + 3:16 + cc * 4 + 4]), reads=[ps, pp], writes=[ac])
                    S.op("pool", lambda: G.tensor_copy(out=sg.t[:, 0:3], in_=halo.t[:, cc * 3:cc * 3 + 3]), reads=[halo], writes=[sg])
                    S.op("pool", lambda: G.tensor_copy(out=halo.t[:, cc * 3:cc * 3 + 3], in_=sg.t[:, 512:515]), reads=[sg], writes=[halo])
                    yield
                    wc = 16 + cc * 4
                    for j in range(3):
                        S.op("dve", lambda j=j: V.scalar_tensor_tensor(out=ac.t[:], in0=sg.t[:, j:j + 512], scalar=pp.t[:, wc + j:wc + j + 1], in1=ac.t[:], op0=ALU.mult, op1=ALU.add), reads=[sg, pp, ac], writes=[ac])
                        yield
                    if kind == 2:
                        S.op("act", lambda: A.activation(out=sbv.t[:], in_=ac.t[:], func=AF.Silu), reads=[ac], writes=[sbv])
                        pT = bfv(nps(pool))
                        for j in range(4):
                            S.op("pe", lambda j=j: P.transpose(out=pT.t[:, j * 128:(j + 1) * 128], in_=sbv.t[:, j * 128:(j + 1) * 128], identity=identb), reads=[sbv, cb], writes=[pT])
                        yield
                        bsel = bbuf.t[:, tt * 16:tt * 16 + 16].rearrange("p (j h) -> p j h", h=4)[:, :, h:h + 1].to_broadcast([128, 4, 128])
                        S.op("dve", lambda: V.tensor_tensor(out=vtok.t[:, :].rearrange("p (j h t) -> p j h t", h=4, t=128)[:, :, h, :], in0=pT.t[:, 0:512].rearrange("p (j t) -> p j t", t=128), in1=bsel, op=ALU.mult), reads=[pT, bbuf], writes=[vtok])
                    else:
                        S.op("act", lambda: A.activation(out=sl.t[:], in_=ac.t[:], func=AF.Silu), reads=[ac], writes=[sl])
                        yield
                        S.op("act", lambda: A.activation(out=sq_.t[:], in_=sl.t[:], func=AF.Square), reads=[sl], writes=[sq_])
                        pn = nps(pool)
                        S.op("pe", lambda: P.matmul(pn.t[:, :], lhsT=onesb, rhs=sq_.t[:], start=True, stop=True), reads=[sq_, cb], writes=[pn])
                        yield
                        S.op("act", lambda: A.activation(out=rs_.t[:], in_=pn.t[:, :], func=AF.Ln, bias=EPS, scale=1.0), reads=[pn], writes=[rs_])
                        yield
                        S.op("act", lambda: A.activation(out=rs_.t[:], in_=rs_.t[:], func=AF.Exp, scale=-0.5), reads=[rs_], writes=[rs_])
                        dst = kT if kind == 1 else qT
                        d3 = dst.t[:, :].rearrange("p (b h t) -> p b h t", h=4, t=128)[:, :, h, :]
                        S.op("dve", lambda: V.scalar_tensor_tensor(out=d3, in0=sl.t[:, :].rearrange("p (b t) -> p b t", t=128), scalar=(1.0 if kind == 1 else 128.0 ** -0.5), in1=rs_.t[:, :].rearrange("p (b t) -> p b t", t=128), op0=ALU.mult, op1=ALU.mult), reads=[sl, rs_], writes=[dst])
                    yield

            class Staged:
                def __init__(self, stages):
                    self.stages = [list(st) for st in stages]

                def step(self):
                    while self.stages and not self.stages[0]:
                        self.stages.pop(0)
                    if not self.stages:
                        return False
                    for g in list(self.stages[0]):
                        try:
                            next(g)
                        except StopIteration:
                            self.stages[0].remove(g)
                    return True

                def done(self):
                    return not any(self.stages)

                def drain(self):
                    while self.step():
                        pass

            def projA(tt):
                return Staged([[projX(tt)], [projC(tt, [4, 8, 0, 6, 10, 2, 12, 14], 0), projC(tt, [5, 9, 1, 7, 11, 3, 13, 15], 1)]])

            def drive2(gens, bg):
                gens = [g for g in gens if g is not None]
                while gens:
                    for g in list(gens):
                        try:
                            next(g)
                        except StopIteration:
                            gens.remove(g)
                    if bg is not None:
                        bg.step()
                        bg.step()

            with contextlib.ExitStack() as ph2:
                negm = sb(ph2, "negm", [128, 512], F32)
                S.op("dve", lambda: V.tensor_scalar(out=negm.t[:], in0=maskS4, scalar1=30000.0, scalar2=-30000.0, op0=ALU.mult, op1=ALU.add), reads=[cf], writes=[negm])
                S32 = sb(ph2, "S32", [128, 512], F32)
                Sbf = sb(ph2, "Sbf", [128, 512], BF16)
                S.op("pool", lambda: G.memset(S32.t[:], 0.0), writes=[S32])
                S.op("pool", lambda: G.memset(Sbf.t[:], 0.0), writes=[Sbf])

                def mk(name, n, shape, dt):
                    return [sb(ph2, "%s%d" % (name, i), shape, dt) for i in range(n)]

                diff = mk("diff", 2, [128, 512], F32)
                Erow = mk("Erow", 2, [128, 512], F32)
                t1 = mk("t1", 2, [128, 512], F32)
                t2 = mk("t2", 2, [128, 512], F32)
                Ms = [mk("Ms%d" % p, 2, [128, 512], F32R) for p in range(2)]
                Ls = [mk("Ls%d" % p, 2, [128, 512], F32R) for p in range(2)]
                Qs = [mk("Qs%d" % p, 2, [128, 512], F32R) for p in range(2)]
                sc = mk("sc", 3, [128, 32], F32)
                qdT = mk("qdT", 3, [128, 512], BF16)
                inT = mk("inT", 3, [128, 512], BF16)
                Tt = mk("Tt", 3, [128, 512], BF16)
                kdec = mk("kdec", 3, [128, 512], BF16)
                Xb = mk("Xb", 1, [128, 512], BF16)
                vn = mk("vn", 1, [128, 512], BF16)
                onb = mk("onb", 1, [128, 512], BF16)
                osq = mk("osq", 1, [128, 512], F32)
                xtmp = osq[0]
                oss = mk("oss", 1, [128, 12], F32)

                def prep(c):
                    p = c % 2
                    i3 = c % 3
                    own = c >= OB0
                    pool = PPs[p]
                    s_ = sc[i3]
                    b4 = bbuf.t[:, c * 4:c * 4 + 4]
                    kT, qT = kTt[(c // 4) % 2], qTt[(c // 4) % 2]
                    cj = c % 4
                    kblk = kT.t[:, cj * 512:(cj + 1) * 512]
                    p0 = nps(pool)
                    S.op("pe", lambda: P.matmul(p0.t[:, 0:4], lhsT=trif, rhs=gbuf.t[:, c * 4:c * 4 + 4], start=True, stop=True), reads=[cf, gbuf], writes=[p0])
                    S.op("dve", lambda: V.tensor_copy(out=s_.t[:, 0:4], in_=p0.t[:, 0:4]), reads=[p0], writes=[s_])
                    pg = nps(pool)
                    for h in range(4):
                        S.op("pe", lambda h=h: P.matmul(pg.t[:, h * 128:(h + 1) * 128], lhsT=gbuf.t[:, c * 4 + h:c * 4 + h + 1].to_broadcast([128, 128]), rhs=trif, start=True, stop=True), reads=[cf, gbuf], writes=[pg])
                    glast = pg.t[:, :].rearrange("p (h t) -> p h t", t=128)[:, :, 127]
                    yield
                    S.op("act", lambda: A.activation(out=s_.t[:, 4:8], in_=s_.t[:, 0:4], func=AF.Exp), reads=[s_], writes=[s_])
                    S.op("dve", lambda: V.scalar_tensor_tensor(out=s_.t[:, 8:12], in0=s_.t[:, 4:8], scalar=-1.0, in1=b4, op0=ALU.mult, op1=ALU.mult), reads=[s_, bbuf], writes=[s_])
                    S.op("dve", lambda: V.tensor_tensor(out=s_.t[:, 20:24], in0=glast, in1=s_.t[:, 0:4], op=ALU.subtract), reads=[pg, s_], writes=[s_])
                    S.op("act", lambda: A.activation(out=s_.t[:, 12:16], in_=s_.t[:, 20:24], func=AF.Exp), reads=[s_], writes=[s_])
                    S.op("act", lambda: A.activation(out=s_.t[:, 16:20], in_=glast, func=AF.Exp), reads=[pg], writes=[s_])
                    df_ = diff[p]
                    S.op("dve", lambda: V.tensor_tensor(out=df_.t[:, :].rearrange("p (h t) -> p h t", t=128), in0=pg.t[:, :].rearrange("p (h t) -> p h t", t=128), in1=s_.t[:, 0:4].unsqueeze(2).to_broadcast([128, 4, 128]), op=ALU.subtract), reads=[pg, s_], writes=[df_])
                    S.op("dve", lambda: V.scalar_tensor_tensor(out=df_.t[:], in0=df_.t[:], scalar=0.0, in1=negm.t[:], op0=ALU.min, op1=ALU.add), reads=[df_, negm], writes=[df_])
                    S.op("act", lambda: A.activation(out=df_.t[:], in_=df_.t[:], func=AF.Exp), reads=[df_], writes=[df_])
                    if own:
                        er = Erow[p]
                        S.op("act", lambda: A.activation(out=er.t[:], in_=pg.t[:, :], func=AF.Exp), reads=[pg], writes=[er])
                        S.op("dve", lambda: V.scalar_tensor_tensor(out=qdT[i3].t[:], in0=qT.t[:, cj * 512:(cj + 1) * 512], scalar=1.0, in1=er.t[:], op0=ALU.mult, op1=ALU.mult), reads=[qT, er], writes=[qdT[i3]])
                    yield
                    pb_ = nps(pool)
                    for h in range(4):
                        S.op("pe", lambda h=h: P.matmul(pb_.t[:, h * 128:(h + 1) * 128], lhsT=bbuf.t[:, c * 4 + h:c * 4 + h + 1].to_broadcast([128, 128]), rhs=identf, start=True, stop=True), reads=[cf, bbuf], writes=[pb_])
                    t1_ = t1[p]
                    S.op("dve", lambda: V.tensor_tensor(out=t1_.t[:], in0=df_.t[:], in1=pb_.t[:, :], op=ALU.mult), reads=[df_, pb_], writes=[t1_])
                    pA = nps(pool)
                    for h in range(4):
                        S.op("pe", lambda h=h: P.matmul(pA.t[:, h * 128:(h + 1) * 128], lhsT=kblk[:, h * 128:(h + 1) * 128], rhs=kblk[:, h * 128:(h + 1) * 128], start=True, stop=True), reads=[kT], writes=[pA])
                    M0 = Ms[p][0]
                    S.op("dve", lambda: V.tensor_tensor(out=M0.t[:], in0=pA.t[:, :], in1=t1_.t[:], op=ALU.mult), reads=[pA, t1_], writes=[M0])
                    yield
                    if own:
                        pQ = nps(pool)
                        for h in range(4):
                            S.op("pe", lambda h=h: P.matmul(pQ.t[:, h * 128:(h + 1) * 128], lhsT=kblk[:, h * 128:(h + 1) * 128], rhs=qT.t[:, cj * 512 + h * 128:cj * 512 + (h + 1) * 128], start=True, stop=True), reads=[kT, qT], writes=[pQ])
                        t2_ = t2[p]
                        S.op("dve", lambda: V.tensor_tensor(out=t2_.t[:], in0=df_.t[:], in1=I4b, op=ALU.add), reads=[df_, cb], writes=[t2_])
                        S.op("dve", lambda: V.tensor_tensor(out=inT[i3].t[:], in0=pQ.t[:, :], in1=t2_.t[:], op=ALU.mult), reads=[pQ, t2_], writes=[inT[i3]])
                    pT = nps(pool)
                    for h in range(4):
                        S.op("pe", lambda h=h: P.transpose(out=pT.t[:, h * 128:(h + 1) * 128], in_=M0.t[:, h * 128:(h + 1) * 128].bitcast(F32), identity=identf), reads=[M0, cf], writes=[pT])
                    Lk = Ls[p][0]
                    S.op("act", lambda: A.activation(out=Lk.t[:], in_=pT.t[:, :], func=AF.Copy), reads=[pT], writes=[Lk])
                    Rk = Qs[p][0]
                    S.op("dve", lambda: V.tensor_tensor(out=Rk.t[:], in0=I4b, in1=M0.t[:], op=ALU.subtract), reads=[M0, cb], writes=[Rk])
                    yield
                    Mk = M0
                    for k in range(6):
                        Ln = Ls[p][(k + 1) % 2]
                        pL = nps(pool)
                        for h in range(4):
                            S.op("pe", lambda h=h: P.matmul(pL.t[:, h * 128:(h + 1) * 128], lhsT=Mk.t[:, h * 128:(h + 1) * 128], rhs=Lk.t[:, h * 128:(h + 1) * 128], start=True, stop=True), reads=[Lk, Mk], writes=[pL])
                        if k < 5:
                            Mn = Ms[p][(k + 1) % 2]
                            pM = nps(pool)
                            for h in range(4):
                                S.op("pe", lambda h=h: P.matmul(pM.t[:, h * 128:(h + 1) * 128], lhsT=Lk.t[:, h * 128:(h + 1) * 128], rhs=Mk.t[:, h * 128:(h + 1) * 128], start=True, stop=True), reads=[Lk, Mk], writes=[pM])
                        S.op("act", lambda: A.activation(out=Ln.t[:], in_=pL.t[:, :], func=AF.Copy), reads=[pL], writes=[Ln])
                        if k < 5:
                            S.op("dve", lambda: V.tensor_copy(out=Mn.t[:], in_=pM.t[:, :]), reads=[pM], writes=[Mn])
                        yield
                        pR = nps(pool)
                        for h in range(4):
                            S.op("pe", lambda h=h: P.matmul(pR.t[:, h * 128:(h + 1) * 128], lhsT=Ln.t[:, h * 128:(h + 1) * 128], rhs=Rk.t[:, h * 128:(h + 1) * 128], start=True, stop=True), reads=[Ln, Rk], writes=[pR])
                        Rn = Qs[p][(k + 1) % 2] if k < 5 else Tt[i3]
                        S.op("dve", lambda: V.tensor_tensor(out=Rn.t[:], in0=pR.t[:, :], in1=Rk.t[:], op=ALU.add), reads=[pR, Rk], writes=[Rn])
                        Lk, Rk = Ln, Rn
                        if k < 5:
                            Mk = Mn
                        yield
                    pT2 = bfv(nps(pool))
                    for h in range(4):
                        S.op("pe", lambda h=h: P.transpose(out=pT2.t[:, h * 128:(h + 1) * 128], in_=kblk[:, h * 128:(h + 1) * 128], identity=identb), reads=[kT, cb], writes=[pT2])
                    S.op("dve", lambda: V.tensor_tensor(out=kdec[i3].t[:, :].rearrange("p (h t) -> p h t", t=128), in0=pT2.t[:, 0:512].rearrange("p (h t) -> p h t", t=128), in1=s_.t[:, 12:16].unsqueeze(2).to_broadcast([128, 4, 128]), op=ALU.mult), reads=[pT2, s_], writes=[kdec[i3]])
                    yield

                def scan(c):
                    i3 = c % 3
                    own = c >= OB0
                    ob = c - OB0
                    s_ = sc[i3]
                    kT, vtok, zsT = kTt[(c // 4) % 2], vtokt[(c // 4) % 2], zsTt[(c // 4) % 2]
                    cj = c % 4
                    kblk = kT.t[:, cj * 512:(cj + 1) * 512]
                    pW = nps(PSC)
                    for h in range(4):
                        S.op("pe", lambda h=h: P.matmul(pW.t[:, h * 128:(h + 1) * 128], lhsT=kblk[:, h * 128:(h + 1) * 128], rhs=Sbf.t[:, h * 128:(h + 1) * 128], start=True, stop=True), reads=[kT, Sbf], writes=[pW])
                    X = Xb[0]
                    S.op("dve", lambda: V.tensor_tensor(out=xtmp.t[:, :].rearrange("p (h t) -> p h t", t=128), in0=pW.t[:, :].rearrange("p (h t) -> p h t", t=128), in1=s_.t[:, 8:12].unsqueeze(2).to_broadcast([128, 4, 128]), op=ALU.mult), reads=[pW, s_], writes=[xtmp])
                    S.op("dve", lambda: V.tensor_tensor(out=X.t[:], in0=xtmp.t[:], in1=vtok.t[:, cj * 512:(cj + 1) * 512], op=ALU.add), reads=[xtmp, vtok], writes=[X])
                    yield
                    pV = nps(PSC)
                    for h in range(4):
                        S.op("pe", lambda h=h: P.matmul(pV.t[:, h * 128:(h + 1) * 128], lhsT=Tt[i3].t[:, h * 128:(h + 1) * 128], rhs=X.t[:, h * 128:(h + 1) * 128], start=True, stop=True), reads=[Tt[i3], X], writes=[pV])
                    vn_ = vn[0]
                    S.op("act", lambda: A.activation(out=vn_.t[:], in_=pV.t[:, :], func=AF.Copy), reads=[pV], writes=[vn_])
                    yield
                    if own:
                        pO = nps(PSC)
                        for h in range(4):
                            S.op("pe", lambda h=h: P.matmul(pO.t[:, h * 128:(h + 1) * 128], lhsT=qdT[i3].t[:, h * 128:(h + 1) * 128], rhs=Sbf.t[:, h * 128:(h + 1) * 128], start=True, stop=False), reads=[qdT[i3], Sbf], writes=[pO])
                            S.op("pe", lambda h=h: P.matmul(pO.t[:, h * 128:(h + 1) * 128], lhsT=inT[i3].t[:, h * 128:(h + 1) * 128], rhs=vn_.t[:, h * 128:(h + 1) * 128], start=False, stop=True), reads=[inT[i3], vn_], writes=[pO])
                    pS = nps(PSC)
                    for h in range(4):
                        S.op("pe", lambda h=h: P.matmul(pS.t[:, h * 128:(h + 1) * 128], lhsT=kdec[i3].t[:, h * 128:(h + 1) * 128], rhs=vn_.t[:, h * 128:(h + 1) * 128], start=True, stop=True), reads=[kdec[i3], vn_], writes=[pS])
                    S.op("dve", lambda: V.tensor_tensor(out=S32.t[:, :].rearrange("p (h t) -> p h t", t=128), in0=S32.t[:, :].rearrange("p (h t) -> p h t", t=128), in1=s_.t[:, 16:20].unsqueeze(2).to_broadcast([128, 4, 128]), op=ALU.mult), reads=[S32, s_], writes=[S32])
                    S.op("dve", lambda: V.tensor_tensor(out=S32.t[:], in0=S32.t[:], in1=pS.t[:, :], op=ALU.add), reads=[S32, pS], writes=[S32])
                    S.op("act", lambda: A.activation(out=Sbf.t[:], in_=S32.t[:], func=AF.Copy), reads=[S32], writes=[Sbf])
                    yield
                    if own:
                        os_, oq = oss[0], osq[0]
                        for h in range(4):
                            S.op("act", lambda h=h: A.activation(out=oq.t[:, h * 128:(h + 1) * 128], in_=pO.t[:, h * 128:(h + 1) * 128], func=AF.Square, accum_out=os_.t[:, h:h + 1]), reads=[pO], writes=[oq, os_], nosame=(h > 0))
                        S.op("act", lambda: A.activation(out=os_.t[:, 4:8], in_=os_.t[:, 0:4], func=AF.Ln, bias=EPS, scale=1.0 / 128), reads=[os_], writes=[os_])
                        S.op("act", lambda: A.activation(out=os_.t[:, 8:12], in_=os_.t[:, 4:8], func=AF.Exp, scale=-0.5), reads=[os_], writes=[os_])
                        yield
                        on_ = onb[0]
                        S.op("dve", lambda: V.tensor_tensor(out=on_.t[:, :].rearrange("p (h t) -> p h t", t=128), in0=pO.t[:, :].rearrange("p (h t) -> p h t", t=128), in1=os_.t[:, 8:12].unsqueeze(2).to_broadcast([128, 4, 128]), op=ALU.mult), reads=[pO, os_], writes=[on_])
                        pT = bfv(nps(PSC))
                        for h in range(4):
                            S.op("pe", lambda h=h: P.transpose(out=pT.t[:, h * 128:(h + 1) * 128], in_=on_.t[:, h * 128:(h + 1) * 128], identity=identb), reads=[on_, cb], writes=[pT])
                        S.op("dve", lambda: V.tensor_tensor(out=mixA.t[:, ob * 512:(ob + 1) * 512], in0=pT.t[:, 0:512], in1=zsT.t[:, cj * 512:(cj + 1) * 512], op=ALU.mult), reads=[pT, zsT], writes=[mixA])
                    yield

                projA(0).drain()
                bg = projA(1)
                preps = {0: prep(0), 1: prep(1)}
                drain(preps.pop(0))
                for c in range(NB):
                    tt, j = c // 4, c % 4
                    if j == 0 and c > 0:
                        bg = projA(tt + 1) if tt + 1 < 8 else None
                    nxt2 = c + 2
                    gate = nxt2 < NB and nxt2 % 4 == 0
                    req = [scan(c)] + ([preps[c + 1]] if c + 1 in preps else [])
                    while req:
                        for g in list(req):
                            try:
                                next(g)
                            except StopIteration:
                                req.remove(g)
                        if nxt2 < NB and nxt2 not in preps and (not gate or bg is None or bg.done()):
                            preps[nxt2] = prep(nxt2)
                        opt = preps.get(nxt2)
                        if opt is not None:
                            try:
                                next(opt)
                            except StopIteration:
                                pass
                        if bg is not None:
                            bg.step()
                            bg.step()
                    if nxt2 < NB and nxt2 not in preps:
                        if bg is not None:
                            bg.drain()
                        preps[nxt2] = prep(nxt2)
                    if gate:
                        bg = None
                    preps.pop(c + 1, None)
                if bg is not None:
                    bg.drain()
            S.barrier()

        mixB = sb(es, "mixB", [128, NOB * 512], BF16)
        if stop_after == "dn":
            dbg_out["mixA"] = (mixA, [128, NOB * 512], BF16)

        def norm_from(xt, sq, ss, xn, pT, dst, dst_ap3, src3=None):
            S.op("act", lambda: A.activation(out=sq.t[:], in_=xt.t[:], func=AF.Square, accum_out=ss.t[:, 0:1]), reads=[xt], writes=[sq, ss])
            S.op("act", lambda: A.activation(out=ss.t[:, 1:2], in_=ss.t[:, 0:1], func=AF.Ln, bias=EPS, scale=1.0 / D), reads=[ss], writes=[ss])
            S.op("act", lambda: A.activation(out=ss.t[:, 2:3], in_=ss.t[:, 1:2], func=AF.Exp, scale=-0.5), reads=[ss], writes=[ss])
            S.op("dve", lambda: V.scalar_tensor_tensor(out=xn.t[:], in0=xt.t[:], scalar=ss.t[:, 2:3], in1=wrow[0].t[:], op0=ALU.mult, op1=ALU.mult), reads=[xt, ss, wrow[0]], writes=[xn])
            for k in range(8):
                S.op("pe", lambda k=k: P.transpose(out=pT.t[:, k * 128:(k + 1) * 128], in_=xn.t[:, k * 128:(k + 1) * 128], identity=identb), reads=[xn, cb], writes=[pT])
            src = pT.t[:, :].rearrange("p (k t) -> p k t", k=8) if src3 is None else src3
            S.op("act", lambda: A.activation(out=dst_ap3, in_=src, func=AF.Copy), reads=[pT], writes=[dst])

        if stop_after in (None, "df"):
            with contextlib.ExitStack() as ph:
                qTd = sb(ph, "qTd", [128, 4 * 2176], BF16)
                kTd = sb(ph, "kTd", [128, 4 * 4096], BF16)
                Vd = sb(ph, "Vd", [128, NB * 512], BF16)
                with contextlib.ExitStack() as p1:
                    set_psum(p1, 6, 2)
                    Wdf = sb(p1, "Wdf", [128, 8 * 1536], BF16)
                    xnT2 = [sb(p1, "xnT%d" % i, [128, 8 * 512], BF16) for i in range(2)]
                    _sq = sb(p1, "sq", [128, D], BF16)
                    _xn = sb(p1, "xn", [128, D], BF16)
                    xbufs = [(sb(p1, "xt%d" % i, [128, D], F32), _sq, sb(p1, "ss%d" % i, [128, 4], F32), _xn) for i in range(1)]
                    wrow[0] = sb(p1, "w1r", [128, D], F32)
                    S.dma(wrow[0].t[:], nw_d[0:1, :].partition_broadcast(128), writes=[wrow[0]])
                    posi = sb(p1, "posi", [128, 512], I32)
                    ni = sb(p1, "ni", [128, 512], I32)
                    Stab = sb(p1, "Stab", [128, 512], F32)
                    Ctab = sb(p1, "Ctab", [128, 512], F32)
                    pr = [sb(p1, "pr%d" % i, [128, 512], F32) for i in range(2)]
                    sqd = [sb(p1, "sqd%d" % i, [128, 512], BF16) for i in range(2)]
                    rsd = [sb(p1, "rsd%d" % i, [128, 512], F32) for i in range(2)]
                    qn = [sb(p1, "qn%d" % i, [128, 512], BF16) for i in range(2)]
                    u1 = [sb(p1, "u1%d" % i, [128, 512], F32) for i in range(2)]
                    u2 = [sb(p1, "u2%d" % i, [128, 512], F32) for i in range(2)]
                    rr_, r2_, nf = u1[0], u1[1], u2[0]

                    def xtile_gen(tt):
                        xnT_ = xnT2[tt % 2]
                        for j in range(4):
                            blk = tt * 4 + j
                            xt, sq, ss, xn = xbufs[j % len(xbufs)]
                            dst3 = xnT_.t[:, :].rearrange("p (k t) -> p k t", k=8)[:, :, j * 128:(j + 1) * 128]
                            norm_block(xl[blk * 128:(blk + 1) * 128, :], xt, sq, ss, xn, psb[j % 2], xnT_, dst3)
                            yield

                    drain(xtile_gen(0))
                    wdma_g(Wdf, w_in, 8, 2056, 1536, [(0, 512), (512, 1024), (1024, 1536)], order=[1, 2, 0])
                    for tt in range(8):
                        xnT = xnT2[tt % 2]
                        S.dma(posi.t[:], posl[0:1, tt * 512:(tt + 1) * 512].partition_broadcast(128), writes=[posi])
                        S.op("dve", lambda: V.tensor_copy(out=rr_.t[:], in_=posi.t[:]), reads=[posi], writes=[rr_])
                        S.op("dve", lambda: V.tensor_scalar(out=rr_.t[:], in0=rr_.t[:], scalar1=pp.t[:, 68:69], scalar2=1.0 / TWO_PI, op0=ALU.mult, op1=ALU.mult), reads=[rr_, pp], writes=[rr_])
                        for (tab, shift) in ((Stab, 0.0), (Ctab, 0.25)):
                            S.op("dve", lambda shift=shift: V.tensor_scalar(out=r2_.t[:], in0=rr_.t[:], scalar1=shift, scalar2=None, op0=ALU.add), reads=[rr_], writes=[r2_])
                            S.op("dve", lambda: V.tensor_copy(out=ni.t[:], in_=r2_.t[:]), reads=[r2_], writes=[ni])
                            S.op("dve", lambda: V.tensor_copy(out=nf.t[:], in_=ni.t[:]), reads=[ni], writes=[nf])
                            S.op("dve", lambda: V.tensor_tensor(out=r2_.t[:], in0=r2_.t[:], in1=nf.t[:], op=ALU.subtract), reads=[r2_, nf], writes=[r2_])
                            S.op("act", lambda tab=tab: A.activation(out=tab.t[:], in_=r2_.t[:], func=AF.Sin, scale=6.2831), reads=[r2_], writes=[tab])
                        def dfchain(ccs, i2, pool):
                            for cc in ccs:
                                isq, hp = cc < 4, cc % 4
                                if isq and tt < 3:
                                    continue
                                ps = nps(pool)
                                proj_fm(ps, Wdf, 1536, cc * 128, xnT)
                                S.op("act", lambda: A.activation(out=sqd[i2].t[:], in_=ps.t[:, :], func=AF.Square), reads=[ps], writes=[sqd[i2]])
                                yield
                                pn = nps(pool)
                                S.op("pe", lambda: P.matmul(pn.t[:, :], lhsT=blk64, rhs=sqd[i2].t[:], start=True, stop=True), reads=[sqd[i2], cb], writes=[pn])
                                S.op("act", lambda: A.activation(out=rsd[i2].t[:], in_=pn.t[:, :], func=AF.Ln, bias=EPS, scale=1.0), reads=[pn], writes=[rsd[i2]])
                                yield
                                S.op("act", lambda: A.activation(out=rsd[i2].t[:], in_=rsd[i2].t[:], func=AF.Exp, scale=-0.5), reads=[rsd[i2]], writes=[rsd[i2]])
                                yield
                                wcol = 65 if isq else 66
                                S.op("dve", lambda: V.scalar_tensor_tensor(out=qn[i2].t[:], in0=ps.t[:, :], scalar=pp.t[:, wcol:wcol + 1], in1=rsd[i2].t[:], op0=ALU.mult, op1=ALU.mult), reads=[ps, pp, rsd[i2]], writes=[qn[i2]])
                                pq = nps(pool)
                                S.op("pe", lambda: P.matmul(pq.t[:, :], lhsT=rotT, rhs=qn[i2].t[:], start=True, stop=True), reads=[qn[i2], cb], writes=[pq])
                                yield
                                S.op("dve", lambda: V.tensor_tensor(out=u1[i2].t[:], in0=pq.t[:, :], in1=Stab.t[:], op=ALU.mult), reads=[pq, Stab], writes=[u1[i2]])
                                S.op("dve", lambda: V.tensor_tensor(out=u2[i2].t[:], in0=qn[i2].t[:], in1=Ctab.t[:], op=ALU.mult), reads=[qn[i2], Ctab], writes=[u2[i2]])
                                yield
                                if isq:
                                    if tt == 3:
                                        o_ap, a0 = qTd.t[:, hp * 2176:hp * 2176 + 128], 384
                                    else:
                                        o_ap, a0 = qTd.t[:, hp * 2176 + 128 + (tt - 4) * 512:hp * 2176 + 128 + (tt - 3) * 512], 0
                                    dstb = qTd
                                else:
                                    o_ap, a0 = kTd.t[:, hp * 4096 + tt * 512:hp * 4096 + (tt + 1) * 512], 0
                                    dstb = kTd
                                S.op("dve", lambda: V.tensor_tensor(out=o_ap, in0=u1[i2].t[:, a0:512], in1=u2[i2].t[:, a0:512], op=ALU.add), reads=[u1[i2], u2[i2]], writes=[dstb])
                                yield

                        def vchain(pool):
                            for j in range(4):
                                blk = tt * 4 + j
                                ps = nps(pool)
                                proj_tm(ps, 512, Wdf, 1536, 1024, xnT, j)
                                S.op("act", lambda: A.activation(out=Vd.t[:, blk * 512:(blk + 1) * 512], in_=ps.t[:, :], func=AF.Copy), reads=[ps], writes=[Vd])
                                yield

                        drive([dfchain([4, 6, 0, 2], 0, [psf[0], psf[1]]), dfchain([5, 7, 1, 3], 1, [psf[2], psf[3]]), vchain([psf[4], psf[5]]), xtile_gen(tt + 1) if tt + 1 < 8 else None])
                S.barrier()
                with contextlib.ExitStack() as p2:
                    set_psum(p2, 8, 0)
                    NPT = 6
                    LA = 3
                    PT = [sb(p2, "PT%d" % i, [128, 512], BF16) for i in range(NPT)]
                    PaccP = [[sb(p2, "PaP%d%d" % (a, m), [128, 512], BF16) for m in range(2)] for a in range(2)]
                    PaccO = [[sb(p2, "PaO%d%d" % (a, m), [128, 512], BF16) for m in range(2)] for a in range(2)]
                    rec = [sb(p2, "rec%d" % i, [128, 512], F32) for i in range(2)]
                    o0 = sb(p2, "o0", [128, 512], F32)
                    o1 = sb(p2, "o1", [128, 512], F32)
                    osq = sb(p2, "osqd", [128, 512], BF16)
                    S.op("dve", lambda: V.memset(mixB.t[:, 0:512], 0.0), writes=[mixB])
                    pvalf = sb(p2, "pvalf", [128, 128], F32)
                    S.op("dve", lambda: V.tensor_copy(out=pvalf.t[:], in_=pval.t[:]), reads=[pval], writes=[pvalf])
                    pOs = [[psf[0], psf[1]], [psf[2], psf[3]]]
                    SPOOL = [psf[4], psf[5], psf[6], psf[7]]
                    groups = []
                    units = []
                    for hp in range(4):
                        for gq in range(5):
                            if gq == 0:
                                q0, N, kd0, nd = 126, 2, 15, 1
                            else:
                                q0, N, kd0, nd = 128 + 512 * (gq - 1), 512, 16 + 4 * (gq - 1), 4
                            gi = len(groups)
                            groups.append((hp, q0, N, kd0, nd))
                            last = kd0 + nd - 1
                            for kb in range(last + 1):
                                off = 0 if kb < kd0 else (kb - kd0) * 128
                                for m in range(2):
                                    units.append((gi, hp, m, kb, off, N, q0, kb == 0, kb == last, kd0))

                    def finalize(gi):
                        hp, q0, N, kd0, nd = groups[gi]
                        par = gi % 2
                        pO = pOs[par]
                        has_own = (kd0 + nd - 1) >= 16
                        for m in range(2):
                            pr_ = nps(SPOOL)
                            S.op("pe", lambda: P.matmul(pr_.t[:, 0:N], lhsT=pval.t[:, :], rhs=PaccP[par][m].t[:, 0:N], start=True, stop=not has_own), reads=[pval, PaccP[par][m]], writes=[pr_])
                            if has_own:
                                S.op("pe", lambda: P.matmul(pr_.t[:, 0:N], lhsT=onesb, rhs=PaccO[par][m].t[:, 0:N], start=False, stop=True), reads=[cb, PaccO[par][m]], writes=[pr_])
                            S.op("act", lambda: A.activation(out=rec[m].t[:, 0:N], in_=pr_.t[:, 0:N], func=AF.Ln, bias=1e-30, scale=1.0), reads=[pr_], writes=[rec[m]])
                            S.op("act", lambda: A.activation(out=rec[m].t[:, 0:N], in_=rec[m].t[:, 0:N], func=AF.Exp, scale=-1.0), reads=[rec[m]], writes=[rec[m]])
                        S.op("dve", lambda: V.tensor_tensor(out=o0.t[:, 0:N], in0=pO[0].t[:, 0:N], in1=rec[0].t[:, 0:N], op=ALU.mult), reads=[pO[0], rec[0]], writes=[o0])
                        S.op("dve", lambda: V.tensor_tensor(out=o1.t[:, 0:N], in0=pO[1].t[:, 0:N], in1=rec[1].t[:, 0:N], op=ALU.mult), reads=[pO[1], rec[1]], writes=[o1])
                        S.op("dve", lambda: V.scalar_tensor_tensor(out=o0.t[:, 0:N], in0=o1.t[:, 0:N], scalar=misc.t[:, 4:5], in1=o0.t[:, 0:N], op0=ALU.mult, op1=ALU.add), reads=[o0, o1, misc], writes=[o0])
                        S.op("act", lambda: A.activation(out=osq.t[:, 0:N], in_=o0.t[:, 0:N], func=AF.Square), reads=[o0], writes=[osq])
                        pn = nps(SPOOL)
                        S.op("pe", lambda: P.matmul(pn.t[:, 0:N], lhsT=ones128, rhs=osq.t[:, 0:N], start=True, stop=True), reads=[osq, cb], writes=[pn])
                        S.op("act", lambda: A.activation(out=rec[0].t[:, 0:N], in_=pn.t[:, 0:N], func=AF.Ln, bias=EPS, scale=1.0), reads=[pn], writes=[rec[0]])
                        S.op("act", lambda: A.activation(out=rec[0].t[:, 0:N], in_=rec[0].t[:, 0:N], func=AF.Exp, scale=-0.5), reads=[rec[0]], writes=[rec[0]])
                        S.op("dve", lambda: V.tensor_scalar(out=o0.t[:, 0:N], in0=o0.t[:, 0:N], scalar1=pp.t[:, 67:68], scalar2=1.0 - LAM_INIT, op0=ALU.mult, op1=ALU.mult), reads=[o0, pp], writes=[o0])
                        ob0 = q0 // 128
                        if N == 2:
                            S.op("dve", lambda: V.tensor_tensor(out=mixB.t[:, hp * 128 + 126:hp * 128 + 128], in0=o0.t[:, 0:2], in1=rec[0].t[:, 0:2], op=ALU.mult), reads=[o0, rec[0]], writes=[mixB])
                        else:
                            d3 = mixB.t[:, :].rearrange("p (b m t) -> p b m t", m=4, t=128)[:, ob0:ob0 + nd, hp, :]
                            S.op("dve", lambda: V.tensor_tensor(out=d3, in0=o0.t[:, 0:N].rearrange("p (b t) -> p b t", t=128), in1=rec[0].t[:, 0:N].rearrange("p (b t) -> p b t", t=128), op=ALU.mult), reads=[o0, rec[0]], writes=[mixB])

                    def stageA(i):
                        gi, hp, m, kb, off, N, q0, first, last, kd0 = units[i]
                        ps = nps(SPOOL)
                        pt = PT[i % NPT]
                        lo, hi = 64 * m, 64 * m + 64
                        S.op("pe", lambda: P.matmul(ps.t[:, off:N], lhsT=kTd.t[lo:hi, hp * 4096 + kb * 128:hp * 4096 + (kb + 1) * 128], rhs=qTd.t[lo:hi, hp * 2176 + q0 + off:hp * 2176 + q0 + N], start=True, stop=True), reads=[kTd, qTd], writes=[ps])
                        S.op("act", lambda: A.activation(out=pt.t[:, off:N], in_=ps.t[:, off:N], func=AF.Exp, scale=0.125), reads=[ps], writes=[pt])
                        en, E = ("dve", V)
                        if kb >= kd0 and N == 2:
                            S.op(en, lambda: E.tensor_tensor(out=pt.t[:, 0:2], in0=pt.t[:, 0:2], in1=maskIb[:, 126:128], op=ALU.mult), reads=[pt, cb], writes=[pt])
                        elif kb >= kd0:
                            S.op(en, lambda: E.tensor_tensor(out=pt.t[:, off:off + 128], in0=pt.t[:, off:off + 128], in1=maskIb, op=ALU.mult), reads=[pt, cb], writes=[pt])
                        par = gi % 2
                        accb = PaccP[par][m] if kb < 16 else PaccO[par][m]
                        if kb == 0 or kb == 16:
                            S.op(en, lambda: E.tensor_copy(out=accb.t[:, 0:N], in_=pt.t[:, 0:N]), reads=[pt], writes=[accb])
                        else:
                            S.op(en, lambda: E.tensor_tensor(out=accb.t[:, off:N], in0=accb.t[:, off:N], in1=pt.t[:, off:N], op=ALU.add), reads=[pt, accb], writes=[accb])

                    def stageC(i):
                        gi, hp, m, kb, off, N, q0, first, last, kd0 = units[i]
                        pt = PT[i % NPT]
                        pO = pOs[gi % 2][m]
                        S.op("pe", lambda: P.matmul(pO.t[:, off:N], lhsT=Vd.t[:, kb * 512 + hp * 128:kb * 512 + (hp + 1) * 128], rhs=pt.t[:, off:N], start=first, stop=last), reads=[Vd, pt], writes=[pO])
                        if last and m == 1:
                            finalize(gi)

                    nu = len(units)
                    LA2 = 4
                    for i in range(0, min(LA2, nu)):
                        stageA(i)
                    for i in range(0, nu, 2):
                        for d in (0, 1):
                            if i + LA2 + d < nu:
                                stageA(i + LA2 + d)
                        stageC(i)
                        stageC(i + 1)
                S.barrier()
            if stop_after == "df":
                dbg_out["mixA"] = (mixA, [128, NOB * 512], BF16)
                dbg_out["mixB"] = (mixB, [128, NOB * 512], BF16)

        if stop_after is None:
            outblk = [Buf(None, "outblk%d" % i) for i in range(16)]
            with contextlib.ExitStack() as phH:
                hnT = sb(phH, "hnT", [128, 8 * 2050], BF16)
                hnT3 = hnT.t[:, :].rearrange("p (k t) -> p k t", k=8)
                Wup = sb(phH, "Wup", [128, 8 * 2816], BF16)
                Wd = sb(phH, "Wd", [128, 11 * 1024], BF16)

                WupG = [Buf(Wup.t, "WupA"), Buf(Wup.t, "WupB")]
                GSPLIT = 6

                def wupb(i):
                    return WupG[0] if i < GSPLIT else WupG[1]

                def ffn_weights(pas):
                    w3 = Wup.t[:, :].rearrange("p (k c) -> p k c", k=8)
                    for gi, (c0, c1) in enumerate(((0, GSPLIT), (GSPLIT, 11))):
                        for base, dcol in ((pas * 1408, 0), (2816 + pas * 1408, 1408)):
                            s3 = w_up[:, base + c0 * 128:base + c1 * 128].rearrange("(k p) c -> p k c", p=128)
                            S.dma(w3[:, :, dcol + c0 * 128:dcol + c1 * 128], s3, writes=[WupG[gi]], q="pool")
                    wdma(Wd, w_down[pas * 1408:(pas + 1) * 1408, :], 11, 0, 1024)

                with contextlib.ExitStack() as ph:
                    set_psum(ph, 6, 2)
                    Wo = sb(ph, "Wo", [128, 8 * 1024], BF16)
                    wrow[0] = sb(ph, "w2r", [128, D], F32)
                    S.dma(wrow[0].t[:], nw_d[1:2, :].partition_broadcast(128), writes=[wrow[0]])
                    xts = [sb(ph, "xt%d" % i, [128, D], F32) for i in range(2)]
                    hts = [sb(ph, "ht%d" % i, [128, D], F32) for i in range(2)]
                    sqs = [sb(ph, "sq%d" % i, [128, D], BF16) for i in range(2)]
                    xns = [sb(ph, "xn%d" % i, [128, D], BF16) for i in range(2)]
                    sss = [sb(ph, "ss%d" % i, [128, 4], F32) for i in range(2)]
                    wdma_g(Wo, w_out, 8, 0, 1024, [(0, 512), (512, 1024)], order=[0, 1])
                    ffn_weights(0)

                    def dblock(ob):
                        par = ob % 2
                        xt, ht, ss, sq, xn, pT = xts[par], hts[par], sss[par], sqs[par], xns[par], psb[par]
                        pool = [psf[3 * par], psf[3 * par + 1], psf[3 * par + 2]]
                        S.dma(xt.t[:], xl[(OB0 + ob) * 128:(OB0 + ob + 1) * 128, :], writes=[xt])
                        for half in range(2):
                            ps = pool[half]
                            for m in range(8):
                                S.op("pe", lambda m=m: P.matmul(ps.t[:, :], lhsT=(mixA if m < 4 else mixB).t[:, (ob * 4 + m % 4) * 128:(ob * 4 + m % 4 + 1) * 128], rhs=Wo.t[:, m * 1024 + half * 512:m * 1024 + (half + 1) * 512], start=(m == 0), stop=(m == 7)), reads=[mixA, mixB, wb(Wo, half * 512)], writes=[ps])
                            S.op("dve", lambda: V.tensor_tensor(out=ht.t[:, half * 512:(half + 1) * 512], in0=ps.t[:, :], in1=xt.t[:, half * 512:(half + 1) * 512], op=ALU.add), reads=[ps, xt], writes=[ht])
                            yield
                        if ob >= 1:
                            S.dma(out_d[(ob - 1) * 128:ob * 128, :], ht.t[:], reads=[ht], writes=[outblk[ob - 1]], owner=ht, is_output=True)
                            dst3, src3 = hnT3[:, :, 2 + (ob - 1) * 128:2 + ob * 128], None
                        else:
                            dst3, src3 = hnT3[:, :, 0:2], pT.t[:, :].rearrange("p (k t) -> p k t", k=8)[:, :, 126:128]
                        S.op("act", lambda: A.activation(out=sq.t[:], in_=ht.t[:], func=AF.Square, accum_out=ss.t[:, 0:1]), reads=[ht], writes=[sq, ss])
                        yield
                        S.op("act", lambda: A.activation(out=ss.t[:, 1:2], in_=ss.t[:, 0:1], func=AF.Ln, bias=EPS, scale=1.0 / D), reads=[ss], writes=[ss])
                        yield
                        S.op("act", lambda: A.activation(out=ss.t[:, 2:3], in_=ss.t[:, 1:2], func=AF.Exp, scale=-0.5), reads=[ss], writes=[ss])
                        S.op("dve", lambda: V.scalar_tensor_tensor(out=xn.t[:], in0=ht.t[:], scalar=ss.t[:, 2:3], in1=wrow[0].t[:], op0=ALU.mult, op1=ALU.mult), reads=[ht, ss, wrow[0]], writes=[xn])
                        yield
                        for k in range(8):
                            S.op("pe", lambda k=k: P.transpose(out=pT.t[:, k * 128:(k + 1) * 128], in_=xn.t[:, k * 128:(k + 1) * 128], identity=identb), reads=[xn, cb], writes=[pT])
                        src = pT.t[:, :].rearrange("p (k t) -> p k t", k=8) if src3 is None else src3
                        S.op("act", lambda: A.activation(out=dst3, in_=src, func=AF.Copy), reads=[pT], writes=[hnT])
                        yield

                    for ob in range(0, NOB, 2):
                        drive([dblock(ob), dblock(ob + 1) if ob + 1 < NOB else None])
                S.barrier()
                with contextlib.ExitStack() as ph:
                    set_psum(ph, 8, 0)
                    ustg = [sb(ph, "ustg%d" % i, [128, 514], F32) for i in range(2)]
                    acc = [sb(ph, "acc%d" % i, [128, 512], F32) for i in range(2)]
                    sgt = sb(ph, "sg", [128, 512], F32)
                    actT = [[sb(ph, "act%d_%d" % (a, i), [128, 512], BF16) for i in range(11)] for a in range(2)]
                    uhalo = sb(ph, "uhalo", [128, 44], F32)
                    hts = [sb(ph, "ht%d" % i, [128, D], F32) for i in range(4)]
                    for pas in range(2):
                        if pas > 0:
                            ffn_weights(pas)
                        def chunk_pair(tt, i):
                            lcs = [i, i + 11]
                            cs = [pas * 11 + i, pas * 11 + i + 22]
                            if tt == 0:
                                for lc in lcs:
                                    ph_ = nps()
                                    for k in range(8):
                                        S.op("pe", lambda k=k: P.matmul(ph_.t[:, 0:2], lhsT=Wup.t[:, k * 2816 + lc * 128:k * 2816 + (lc + 1) * 128], rhs=hnT.t[:, k * 2050:k * 2050 + 2], start=(k == 0), stop=(k == 7)), reads=[wupb(i), hnT], writes=[ph_])
                                    S.op("act", lambda: A.activation(out=uhalo.t[:, lc * 2:lc * 2 + 2], in_=ph_.t[:, 0:2], func=AF.Copy), reads=[ph_], writes=[uhalo])
                            pss = [nps(), nps()]
                            for w in range(2):
                                ps, lc = pss[w], lcs[w]
                                for k in range(8):
                                    S.op("pe", lambda k=k: P.matmul(ps.t[:, :], lhsT=Wup.t[:, k * 2816 + lc * 128:k * 2816 + (lc + 1) * 128], rhs=hnT.t[:, k * 2050 + 2 + tt * 512:k * 2050 + 2 + (tt + 1) * 512], start=(k == 0), stop=(k == 7)), reads=[wupb(i), hnT], writes=[ps])
                            for w in range(2):
                                S.op("act", lambda w=w: A.activation(out=ustg[w].t[:, 2:514], in_=pss[w].t[:, :], func=AF.Copy), reads=[pss[w]], writes=[ustg[w]])
                            for w in range(2):
                                wc, c = 72 + cs[w] * 3, cs[w]
                                S.op("act", lambda w=w, wc=wc, c=c: A.activation(out=acc[w].t[:], in_=pss[w].t[:, :], func=AF.Identity, scale=pp.t[:, wc + 2:wc + 3], bias=pp.t[:, 204 + c:205 + c]), reads=[pss[w], pp], writes=[acc[w]])
                            for w in range(2):
                                lc = lcs[w]
                                S.op("pool", lambda w=w, lc=lc: G.tensor_copy(out=ustg[w].t[:, 0:2], in_=uhalo.t[:, lc * 2:lc * 2 + 2]), reads=[uhalo], writes=[ustg[w]])
                                S.op("pool", lambda w=w, lc=lc: G.tensor_copy(out=uhalo.t[:, lc * 2:lc * 2 + 2], in_=ustg[w].t[:, 512:514]), reads=[ustg[w]], writes=[uhalo])
                            for j in (0, 1):
                                for w in range(2):
                                    wc = 72 + cs[w] * 3
                                    S.op("dve", lambda w=w, wc=wc, j=j: V.scalar_tensor_tensor(out=acc[w].t[:], in0=ustg[w].t[:, j:j + 512], scalar=pp.t[:, wc + j:wc + j + 1], in1=acc[w].t[:], op0=ALU.mult, op1=ALU.add), reads=[ustg[w], pp, acc[w]], writes=[acc[w]])
                            S.op("act", lambda: A.activation(out=sgt.t[:], in_=acc[0].t[:], func=AF.Silu), reads=[acc[0]], writes=[sgt])
                            aT = actT[tt % 2][i]
                            S.op("dve", lambda: V.tensor_tensor(out=aT.t[:], in0=acc[1].t[:], in1=sgt.t[:], op=ALU.mult), reads=[acc[1], sgt], writes=[aT])

                        def wdown_loads(tt):
                            for j in range(4):
                                ob = tt * 4 + j
                                S.dma(hts[j].t[:], out_d[ob * 128:(ob + 1) * 128, :], reads=[outblk[ob]], writes=[hts[j]])

                        def wdown(tt):
                            for j in range(4):
                                ob = tt * 4 + j
                                ht = hts[j]
                                for half in range(2):
                                    ps = nps()
                                    for i in range(11):
                                        aT = actT[tt % 2][i]
                                        S.op("pe", lambda i=i, aT=aT: P.matmul(ps.t[:, :], lhsT=aT.t[:, j * 128:(j + 1) * 128], rhs=Wd.t[:, i * 1024 + half * 512:i * 1024 + (half + 1) * 512], start=(i == 0), stop=(i == 10)), reads=[aT, Wd], writes=[ps])
                                    S.op("dve", lambda: V.tensor_tensor(out=ht.t[:, half * 512:(half + 1) * 512], in0=ps.t[:, :], in1=ht.t[:, half * 512:(half + 1) * 512], op=ALU.add), reads=[ps, ht], writes=[ht])
                                S.dma(out_d[ob * 128:(ob + 1) * 128, :], ht.t[:], reads=[ht], writes=[outblk[ob]], owner=ht, is_output=True)

                        DEFER = 3
                        for tt in range(4):
                            for i in range(11):
                                if tt > 0 and i == 0:
                                    wdown_loads(tt - 1)
                                chunk_pair(tt, i)
                                if tt > 0 and i == DEFER - 1:
                                    wdown(tt - 1)
                                if tt == 3 and i == 7:
                                    wdown_loads(3)
                        wdown(3)

        dbg_d = {}
        for name, (buf, shape, dt) in dbg_out.items():
            d = nc.dram_tensor("dbg_" + name, shape, dt, kind="ExternalOutput").ap()
            S.dma(d[:, :], buf.t[:], reads=[buf], is_output=True)
        if stop_after is not None:
            zt = sb(es, "zt", [128, D], F32)
            S.op("pool", lambda: G.memset(zt.t[:], 0.0), writes=[zt])
            for r in range(16):
                S.dma(out_d[r * 128:(r + 1) * 128, :], zt.t[:], reads=[zt], is_output=True)
        S.finish()
    return nc, S


def build_rest(nc, S, es, L):
    raise NotImplementedError


def _consts():
    p = np.arange(128)
    ident = np.eye(128, dtype=np.float32)
    tri = (p[:, None] <= p[None, :]).astype(np.float32)
    strict = (p[None, :] > p[:, None]).astype(np.float32)
    cf = np.concatenate([ident, tri, np.ones((128, 128), np.float32), np.tile(strict, (1, 4)), np.tile(tri * np.float32(128.0 ** -0.5), (1, 4))], axis=1)
    rotT = np.zeros((128, 128), np.float32)
    for base in (0, 64):
        for i in range(8):
            rotT[base + i + 8, base + i] = -1.0
            rotT[base + i, base + i + 8] = 1.0
    blk64 = np.zeros((128, 128), np.float32)
    blk64[:64, :64] = 1.0 / 64
    blk64[64:, 64:] = 1.0 / 64
    cb = np.concatenate([ident, np.ones((128, 128), np.float32), tri, rotT, blk64, np.full((128, 128), 1.0 / 128, np.float32), np.tile(ident, (1, 4))], axis=1).astype(ml_dtypes.bfloat16)
    return np.ascontiguousarray(cf), np.ascontiguousarray(cb)


def _pack_params(inp):
    pp = np.zeros((128, 512), np.float32)
    pp[:, 0:8] = inp["norm1_w"][0].reshape(8, 128).T
    pp[:, 8:16] = inp["norm2_w"][0].reshape(8, 128).T
    cw = inp["dn_conv_w"][0]
    pp[:, 16:64] = cw.reshape(4, 12, 128).transpose(2, 1, 0).reshape(128, 48)
    pp[:, 64] = inp["dn_norm_w"][0]
    pp[:, 65] = np.tile(inp["df_q_norm_w"][0], 2)
    pp[:, 66] = np.tile(inp["df_k_norm_w"][0], 2)
    pp[:, 67] = inp["df_subln_w"][0]
    invf = np.zeros(128, np.float32)
    base = (500000.0 ** (-np.arange(0, 16, 2, dtype=np.float32) / 16)).astype(np.float32)
    for b0 in (0, 64):
        invf[b0:b0 + 8] = base
        invf[b0 + 8:b0 + 16] = base
    pp[:, 68] = invf
    fw = inp["ffn_conv_w"][0]
    pp[:, 72:204] = fw.reshape(3, 44, 128).transpose(2, 1, 0).reshape(128, 132)
    pp[:, 204:248] = inp["ffn_conv_b"][0].reshape(44, 128).T
    pp[:, 248:252] = inp["dn_a_log"][0][None, :]
    pp[:, 252:256] = inp["dn_dt_bias"][0][None, :]
    pp[:, 256:320] = inp["df_lambda_q1"][0][None, :]
    pp[:, 320:384] = inp["df_lambda_k1"][0][None, :]
    pp[:, 384:448] = inp["df_lambda_q2"][0][None, :]
    pp[:, 448:512] = inp["df_lambda_k2"][0][None, :]
    return pp


def make_in_maps(inp):
    cf, cb = _consts()
    pp = _pack_params(inp)
    x = np.asarray(inp["x"], np.float32)
    pos = np.asarray(inp["positions"], np.int32)
    shared = {
        "pp": pp, "cf": cf, "cb": cb,
        "nw": np.ascontiguousarray(np.stack([inp["norm1_w"][0], inp["norm2_w"][0]]).astype(np.float32)),
        "w_in": np.ascontiguousarray(inp["w_in"][0]), "w_out": np.ascontiguousarray(inp["w_out"][0]),
        "w_up": np.ascontiguousarray(inp["w_up"][0]), "w_down": np.ascontiguousarray(inp["w_down"][0]),
    }
    maps = []
    for c in range(8):
        b, half = c // 2, c % 2
        if half == 1:
            xloc = x[b]
            ploc = pos[b][None, :]
            pv = np.ones((128, 128), ml_dtypes.bfloat16)
        else:
            xloc = np.concatenate([np.zeros((2048, D), np.float32), x[b, :2048]], axis=0)
            ploc = np.concatenate([np.zeros(2048, np.int32), pos[b, :2048]])[None, :]
            pv = np.zeros((128, 128), ml_dtypes.bfloat16)
        m = dict(shared)
        m["xl"] = np.ascontiguousarray(xloc)
        m["posl"] = np.ascontiguousarray(ploc)
        m["pval"] = pv
        maps.append(m)
    return maps


_CACHE = {}


def kernel(**inputs):
    inp = {k: np.asarray(v) for k, v in inputs.items()}
    if "nc" not in _CACHE:
        _CACHE["nc"] = build_program()[0]
    nc = _CACHE["nc"]
    maps = make_in_maps(inp)
    res = run_bass_kernel_spmd(nc, maps, core_ids=list(range(8)))
    out = np.zeros((4, 4096, D), np.float32)
    for c in range(8):
        b, half = c // 2, c % 2
        out[b, half * 2048:(half + 1) * 2048] = res.results[c]["out"]
    return out
```
